# Optimizing a Trainium2 kernel written in Bass

```python
import math
import jax, jax.numpy as jnp
from jax import lax
import numpy as np

D_MODEL = 2048
BATCH = 1
SEQ = 16384
DEPTH = 1
DEC_BATCH = 1
DEC_SEQ = 8192
PAST_LEN = 128

HEAD_DIM = 128
GDN_HEADS = 8
GDN_W = GDN_HEADS * HEAD_DIM
ATT_Q_HEADS = 8
ATT_KV_HEADS = 2
ATT_GROUP = ATT_Q_HEADS // ATT_KV_HEADS
ATT_Q_W = ATT_Q_HEADS * HEAD_DIM
ATT_KV_W = ATT_KV_HEADS * HEAD_DIM
MIX_W = GDN_W + ATT_Q_W
IN_DIM = 3 * GDN_W + GDN_W + 4 * GDN_HEADS + ATT_Q_W + 2 * ATT_KV_W
D_FF = 5632
CONV_K = 5
CHUNK = 64
Q_BLOCK = 128
GRID_W = 64
AXIS_DIM = HEAD_DIM // 2
ROPE_THETA = 10000.0
EPS = 1e-6

kernel_name = "hybrid_gdn_gqa_axial_macaron_encoder"


def rmsnorm(x, g):
    xf = x.astype(jnp.float32)
    y = xf * lax.rsqrt(jnp.mean(xf * xf, axis=-1, keepdims=True) + EPS)
    return (y * g.astype(jnp.float32)).astype(x.dtype)


def l2norm(x):
    return x * lax.rsqrt(jnp.sum(x * x, axis=-1, keepdims=True) + EPS)


def swiglu(x, w_gate, w_up, w_down):
    return (jax.nn.silu(x @ w_gate) * (x @ w_up)) @ w_down


def centred_conv(x, w):
    pad = CONV_K // 2
    L = x.shape[1]
    xp = jnp.pad(x, ((0, 0), (pad, pad), (0, 0)))
    return sum(xp[:, i:i + L] * w[i] for i in range(CONV_K))


def gated_delta_chunked(q, k, v, g, beta):
    Bd, H, L, dk = q.shape
    dv = v.shape[-1]
    N = L // CHUNK
    q = q.reshape(Bd, H, N, CHUNK, dk)
    k = k.reshape(Bd, H, N, CHUNK, dk)
    v = v.reshape(Bd, H, N, CHUNK, dv)
    g = g.reshape(Bd, H, N, CHUNK)
    beta = beta.reshape(Bd, H, N, CHUNK)
    gc = jnp.cumsum(g, axis=-1)
    tri_incl = jnp.tril(jnp.ones((CHUNK, CHUNK), dtype=bool))
    tri_strict = jnp.tril(jnp.ones((CHUNK, CHUNK), dtype=bool), k=-1)
    diff = gc[..., :, None] - gc[..., None, :]
    decay = jnp.where(tri_incl, jnp.exp(jnp.where(tri_incl, diff, 0.0)), 0.0)
    kb = k * beta[..., None]
    vb = v * beta[..., None]
    a_strict = jnp.where(tri_strict, jnp.einsum('bhncd,bhnsd->bhncs', kb, k) * decay, 0.0)
    rhs = jnp.concatenate([vb, kb * jnp.exp(gc)[..., None]], axis=-1)
    sol = lax.linalg.triangular_solve(a_strict, rhs, left_side=True, lower=True, unit_diagonal=True)
    u = sol[..., :dv]
    w = sol[..., dv:]
    attn_qk = jnp.einsum('bhncd,bhnsd->bhncs', q, k) * decay
    q_dec = q * jnp.exp(gc)[..., None]
    k_dec = k * jnp.exp(gc[..., -1:] - gc)[..., None]
    chunk_decay = jnp.exp(gc[..., -1])

    def step(S, inp):
        u_c, w_c, qd_c, kd_c, aqk_c, cd_c = inp
        v_new = u_c - jnp.einsum('bhcd,bhde->bhce', w_c, S)
        o_c = jnp.einsum('bhcd,bhde->bhce', qd_c, S) + jnp.einsum('bhcs,bhse->bhce', aqk_c, v_new)
        S = S * cd_c[..., None, None] + jnp.einsum('bhcd,bhce->bhde', kd_c, v_new)
        return S, o_c

    xs = tuple(jnp.moveaxis(t, 2, 0) for t in (u, w, q_dec, k_dec, attn_qk, chunk_decay))
    S0 = jnp.zeros((Bd, H, dk, dv), jnp.float32)
    _, o = lax.scan(step, S0, xs)
    return jnp.moveaxis(o, 0, 2).reshape(Bd, H, L, dv)


def axial_rope_tables(rows):
    row = jnp.repeat(jnp.arange(rows, dtype=jnp.float32), GRID_W)
    col = jnp.tile(jnp.arange(GRID_W, dtype=jnp.float32), rows)
    freqs = ROPE_THETA ** (-jnp.arange(0, AXIS_DIM, 2, dtype=jnp.float32) / AXIS_DIM)
    ang_r = row[:, None] * freqs[None, :]
    ang_c = col[:, None] * freqs[None, :]
    return jnp.cos(ang_r), jnp.sin(ang_r), jnp.cos(ang_c), jnp.sin(ang_c)


def rotate(x, cos, sin):
    half = x.shape[-1] // 2
    x1, x2 = x[..., :half], x[..., half:]
    c = cos[None, :, None, :].astype(x.dtype)
    s = sin[None, :, None, :].astype(x.dtype)
    return jnp.concatenate([x1 * c - x2 * s, x2 * c + x1 * s], axis=-1)


def apply_axial_rope(x, tables):
    cr, sr, cc, sc = tables
    return jnp.concatenate([rotate(x[..., :AXIS_DIM], cr, sr), rotate(x[..., AXIS_DIM:], cc, sc)], axis=-1)


def block_attention(q, k, v):
    B, L, _, dh = q.shape
    nb = L // Q_BLOCK
    qb = q.reshape(B, nb, Q_BLOCK, ATT_KV_HEADS, ATT_GROUP, dh).transpose(1, 0, 3, 4, 2, 5)
    kt = k.transpose(0, 2, 1, 3)
    vt = v.transpose(0, 2, 1, 3)
    scale = HEAD_DIM ** -0.5

    def one_block(qi):
        s = jnp.einsum('bhgqd,bhkd->bhgqk', qi, kt).astype(jnp.float32) * scale
        p = jax.nn.softmax(s, axis=-1).astype(vt.dtype)
        return jnp.einsum('bhgqk,bhkd->bhgqd', p, vt)

    o = lax.map(one_block, qb)
    return o.transpose(1, 0, 4, 2, 3, 5).reshape(B, L, ATT_Q_HEADS, dh)


def token_mixer(u, rope_tables, w_in, conv_w, a_log_fwd, a_log_bwd, dt_bias_fwd, dt_bias_bwd,
                gdn_out_norm, q_norm, k_norm, attn_out_norm, w_out):
    B, L, _ = u.shape
    proj = u @ w_in
    splits = np.cumsum([3 * GDN_W, GDN_W, 2 * GDN_HEADS, 2 * GDN_HEADS, ATT_Q_W, ATT_KV_W]).tolist()
    qkv_g, z, a, b, q_a, k_a, v_a = jnp.split(proj, splits, axis=-1)

    qkv_g = jax.nn.silu(centred_conv(qkv_g, conv_w)).astype(jnp.float32)
    qg, kg, vg = jnp.split(qkv_g, 3, axis=-1)
    to_heads = lambda t: t.reshape(B, L, GDN_HEADS, HEAD_DIM).transpose(0, 2, 1, 3)
    qg = l2norm(to_heads(qg)) * (HEAD_DIM ** -0.5)
    kg = l2norm(to_heads(kg))
    vg = to_heads(vg)
    a = a.astype(jnp.float32)
    b = b.astype(jnp.float32)
    g_f = -jnp.exp(a_log_fwd.astype(jnp.float32)) * jax.nn.softplus(a[..., :GDN_HEADS] + dt_bias_fwd.astype(jnp.float32))
    g_b = -jnp.exp(a_log_bwd.astype(jnp.float32)) * jax.nn.softplus(a[..., GDN_HEADS:] + dt_bias_bwd.astype(jnp.float32))
    beta_f = jax.nn.sigmoid(b[..., :GDN_HEADS])
    beta_b = jax.nn.sigmoid(b[..., GDN_HEADS:])
    to_bhl = lambda t: t.transpose(0, 2, 1)
    flip = lambda t: jnp.flip(t, axis=2)
    q2 = jnp.concatenate([qg, flip(qg)], axis=0)
    k2 = jnp.concatenate([kg, flip(kg)], axis=0)
    v2 = jnp.concatenate([vg, flip(vg)], axis=0)
    g2 = jnp.concatenate([to_bhl(g_f), flip(to_bhl(g_b))], axis=0)
    beta2 = jnp.concatenate([to_bhl(beta_f), flip(to_bhl(beta_b))], axis=0)
    o2 = gated_delta_chunked(q2, k2, v2, g2, beta2)
    o_g = (o2[:B] + flip(o2[B:])).transpose(0, 2, 1, 3)
    z_h = z.reshape(B, L, GDN_HEADS, HEAD_DIM).astype(jnp.float32)
    o_g = rmsnorm(o_g, gdn_out_norm) * jax.nn.silu(z_h)
    o_g = o_g.reshape(B, L, GDN_W).astype(u.dtype)

    qa = rmsnorm(q_a.reshape(B, L, ATT_Q_HEADS, HEAD_DIM), q_norm)
    ka = rmsnorm(k_a.reshape(B, L, ATT_KV_HEADS, HEAD_DIM), k_norm)
    va = v_a.reshape(B, L, ATT_KV_HEADS, HEAD_DIM)
    qa = apply_axial_rope(qa, rope_tables)
    ka = apply_axial_rope(ka, rope_tables)
    o_a = block_attention(qa, ka, va)
    o_a = rmsnorm(o_a, attn_out_norm).reshape(B, L, ATT_Q_W).astype(u.dtype)

    return jnp.concatenate([o_g, o_a], axis=-1) @ w_out


def encode(x, rows, ffn1_norm, ffn1_w_gate, ffn1_w_up, ffn1_w_down, mix_norm, w_in, conv_w,
           a_log_fwd, a_log_bwd, dt_bias_fwd, dt_bias_bwd, gdn_out_norm, q_norm, k_norm,
           attn_out_norm, w_out, ffn2_norm, ffn2_w_gate, ffn2_w_up, ffn2_w_down, final_norm):
    rope_tables = axial_rope_tables(rows)
    h = x
    for l in range(DEPTH):
        h = h + 0.5 * swiglu(rmsnorm(h, ffn1_norm[l]), ffn1_w_gate[l], ffn1_w_up[l], ffn1_w_down[l])
        h = h + token_mixer(rmsnorm(h, mix_norm[l]), rope_tables, w_in[l], conv_w[l],
                            a_log_fwd[l], a_log_bwd[l], dt_bias_fwd[l], dt_bias_bwd[l],
                            gdn_out_norm[l], q_norm[l], k_norm[l], attn_out_norm[l], w_out[l])
        h = h + 0.5 * swiglu(rmsnorm(h, ffn2_norm[l]), ffn2_w_gate[l], ffn2_w_up[l], ffn2_w_down[l])
    return rmsnorm(h, final_norm)


def setup_inputs(seed: int = 0) -> dict:
    key = jax.random.key(seed)
    ks = jax.random.split(key, 32)
    f32 = jnp.float32

    def w(k, shape, fan_in):
        return jax.random.normal(k, shape, f32) * (fan_in ** -0.5)

    def gain(k, shape):
        return 1.0 + 0.02 * jax.random.normal(k, shape, f32)

    def a_log(k):
        return jnp.log(jax.random.uniform(k, (DEPTH, GDN_HEADS), f32, 1.0, 16.0))

    def dt_bias(k):
        u = jax.random.uniform(k, (DEPTH, GDN_HEADS), f32)
        dt = jnp.exp(u * (math.log(0.1) - math.log(0.001)) + math.log(0.001))
        return dt + jnp.log(-jnp.expm1(-dt))

    return {
        "x_prompt": jax.random.normal(ks[0], (BATCH, SEQ, D_MODEL), f32),
        "x_sample": jax.random.normal(ks[1], (DEC_BATCH, DEC_SEQ, D_MODEL), f32),
        "ffn1_norm": gain(ks[2], (DEPTH, D_MODEL)),
        "ffn1_w_gate": w(ks[3], (DEPTH, D_MODEL, D_FF), D_MODEL),
        "ffn1_w_up": w(ks[4], (DEPTH, D_MODEL, D_FF), D_MODEL),
        "ffn1_w_down": w(ks[5], (DEPTH, D_FF, D_MODEL), D_FF),
        "mix_norm": gain(ks[6], (DEPTH, D_MODEL)),
        "w_in": w(ks[7], (DEPTH, D_MODEL, IN_DIM), D_MODEL),
        "conv_w": w(ks[8], (DEPTH, CONV_K, 3 * GDN_W), CONV_K),
        "a_log_fwd": a_log(ks[9]),
        "a_log_bwd": a_log(ks[10]),
        "dt_bias_fwd": dt_bias(ks[11]),
        "dt_bias_bwd": dt_bias(ks[12]),
        "gdn_out_norm": gain(ks[13], (DEPTH, HEAD_DIM)),
        "q_norm": gain(ks[14], (DEPTH, HEAD_DIM)),
        "k_norm": gain(ks[15], (DEPTH, HEAD_DIM)),
        "attn_out_norm": gain(ks[16], (DEPTH, HEAD_DIM)),
        "w_out": w(ks[17], (DEPTH, MIX_W, D_MODEL), MIX_W),
        "ffn2_norm": gain(ks[18], (DEPTH, D_MODEL)),
        "ffn2_w_gate": w(ks[19], (DEPTH, D_MODEL, D_FF), D_MODEL),
        "ffn2_w_up": w(ks[20], (DEPTH, D_MODEL, D_FF), D_MODEL),
        "ffn2_w_down": w(ks[21], (DEPTH, D_FF, D_MODEL), D_FF),
        "final_norm": gain(ks[22], (D_MODEL,)),
    }


def reference(x_prompt, x_sample, ffn1_norm, ffn1_w_gate, ffn1_w_up, ffn1_w_down, mix_norm, w_in, conv_w,
              a_log_fwd, a_log_bwd, dt_bias_fwd, dt_bias_bwd, gdn_out_norm, q_norm, k_norm,
              attn_out_norm, w_out, ffn2_norm, ffn2_w_gate, ffn2_w_up, ffn2_w_down, final_norm):
    rows_prompt = x_prompt.shape[1] // GRID_W
    rows_sample = x_sample.shape[1] // GRID_W
    y_prompt = encode(x_prompt, rows_prompt, ffn1_norm, ffn1_w_gate, ffn1_w_up, ffn1_w_down, mix_norm, w_in,
                      conv_w, a_log_fwd, a_log_bwd, dt_bias_fwd, dt_bias_bwd, gdn_out_norm, q_norm, k_norm,
                      attn_out_norm, w_out, ffn2_norm, ffn2_w_gate, ffn2_w_up, ffn2_w_down, final_norm)
    y_sample = encode(x_sample, rows_sample, ffn1_norm, ffn1_w_gate, ffn1_w_up, ffn1_w_down, mix_norm, w_in,
                      conv_w, a_log_fwd, a_log_bwd, dt_bias_fwd, dt_bias_bwd, gdn_out_norm, q_norm, k_norm,
                      attn_out_norm, w_out, ffn2_norm, ffn2_w_gate, ffn2_w_up, ffn2_w_down, final_norm)
    return (y_prompt, y_sample)
```

```python
import numpy as np
from contextlib import ExitStack
import concourse.bass as bass
import concourse.mybir as mybir
from concourse.bass_utils import run_bass_kernel_spmd

F32 = mybir.dt.float32
BF16 = mybir.dt.bfloat16
I32 = mybir.dt.int32
AF = mybir.ActivationFunctionType
ALU = mybir.AluOpType
AX = mybir.AxisListType

D = 2048
DFF = 5632
NFF = DFF // 128
IN_DIM = 5664
HD = 128
NCORE = 8
EPS = 1e-6
KD = D // 128
GDN_INTERLEAVE = True


class Res:
    __slots__ = ("w", "r", "name", "excl")

    def __init__(self, name="", excl=False):
        self.w = {}
        self.r = {}
        self.name = name
        self.excl = excl


class Sched:
    ENG = ("sp", "act", "pe", "dve", "pool")

    def __init__(self, nc, stack, n_dma_sp=24, n_dma_pool=24):
        self.nc = nc
        self.lists = {k: [] for k in self.ENG}
        self.sem = {}
        self.cnt = {}
        for k in self.ENG:
            self.sem[k] = stack.enter_context(nc.semaphore("s_" + k))
            self.cnt[k] = 0
        self.dma_slots = {"sp": [], "pool": []}
        for q, n in (("sp", n_dma_sp), ("pool", n_dma_pool)):
            for i in range(n):
                key = "d_%s_%d" % (q, i)
                self.sem[key] = stack.enter_context(nc.semaphore(key))
                self.cnt[key] = 0
                self.dma_slots[q].append(key)
        self.dma_rr = {"sp": 0, "pool": 0}
        self.known = {k: {} for k in self.ENG}
        self.block = None

    def _wait(self, eng, deps):
        kn = self.known[eng]
        for k, v in deps.items():
            if v <= 0:
                continue
            if k == eng and eng == "pe":
                continue
            if kn.get(k, 0) >= v:
                continue
            kn[k] = v
            sem = self.sem[k]
            self.lists[eng].append(lambda e, sem=sem, v=v: e.wait_ge(sem, v))

    @staticmethod
    def _merge(d, k, v):
        if d.get(k, 0) < v:
            d[k] = v

    def op(self, eng, fn, reads=(), writes=(), pwrites=(), dma=False):
        deps = {}
        for r in reads:
            for k, v in r.w.items():
                self._merge(deps, k, v)
            if r.excl:
                for k, v in r.r.items():
                    if k != eng:
                        self._merge(deps, k, v)
        for w in writes:
            for k, v in w.w.items():
                self._merge(deps, k, v)
            for k, v in w.r.items():
                self._merge(deps, k, v)
        for w in pwrites:
            for k, v in w.r.items():
                self._merge(deps, k, v)
        if dma:
            slots = self.dma_slots[eng]
            key = slots[self.dma_rr[eng] % len(slots)]
            self.dma_rr[eng] += 1
            self._merge(deps, key, self.cnt[key])
            self._wait(eng, deps)
            self.cnt[key] += 16
            ev = (key, self.cnt[key])
            sem = self.sem[key]
            self.lists[eng].append(lambda e, fn=fn, sem=sem: fn(e).then_inc(sem, 16))
        else:
            self._wait(eng, deps)
            self.cnt[eng] += 1
            ev = (eng, self.cnt[eng])
            sem = self.sem[eng]
            self.lists[eng].append(lambda e, fn=fn, sem=sem: fn(e).then_inc(sem, 1))
        for r in reads:
            self._merge(r.r, ev[0], ev[1])
        for w in writes:
            w.w = {ev[0]: ev[1]}
            w.r = {}
        for w in pwrites:
            self._merge(w.w, ev[0], ev[1])
        return ev

    def all_events(self):
        return {k: v for k, v in self.cnt.items() if v > 0}

    def barrier(self):
        ev = self.all_events()
        for eng in self.ENG:
            self._wait(eng, dict(ev))

    def flush(self):
        blk = self.block
        for key, dec in (("sp", blk.sync), ("act", blk.scalar), ("pe", blk.tensor),
                         ("dve", blk.vector), ("pool", blk.gpsimd)):
            lst = self.lists[key]
            if not lst:
                continue

            def body(e, lst=lst):
                for th in lst:
                    th(e)
            dec(body)
            self.lists[key] = []


class Ctx:
    pass


_UID = [0]


def _tile(stack, nc, name, shape, dt):
    _UID[0] += 1
    t = stack.enter_context(nc.sbuf_tensor("%s_%d" % (name, _UID[0]), list(shape), dt))
    return t


def _psum(stack, nc, name, shape, dt):
    _UID[0] += 1
    return stack.enter_context(nc.psum_tensor("%s_%d" % (name, _UID[0]), list(shape), dt))


def emit_norm_T(C, src, src_res, gcols, k_list, dstT, dst_res, col0, pbanks, width=D, g_res=None):
    S = C.S
    nk = width // 128
    ss, rs, xs, junk = C.n_ss, C.n_rs, C.n_xs, C.n_junk
    S.op("act", lambda e: e.activation(out=junk[:, 0:width], in_=src[:, 0:width], func=AF.Square,
                                       accum_out=ss[:, 0:1]),
         reads=[src_res], writes=[C.r_junk, C.r_ss])
    S.op("dve", lambda e: e.tensor_scalar(out=rs[:, 0:1], in0=ss[:, 0:1], scalar1=1.0 / width, scalar2=EPS,
                                          op0=ALU.mult, op1=ALU.add), reads=[C.r_ss], writes=[C.r_rs])
    S.op("act", lambda e: e.activation(out=rs[:, 0:1], in_=rs[:, 0:1], func=AF.Sqrt),
         reads=[C.r_rs], writes=[C.r_rs])
    S.op("dve", lambda e: e.reciprocal(out=rs[:, 0:1], in_=rs[:, 0:1]), reads=[C.r_rs], writes=[C.r_rs])
    S.op("dve", lambda e: e.tensor_scalar(out=xs[:, 0:width], in0=src[:, 0:width], scalar1=rs[:, 0:1],
                                          scalar2=None, op0=ALU.mult),
         reads=[src_res, C.r_rs], writes=[C.r_xs])
    import os
    for kq in range(int(os.environ.get('NKQ', nk // 4))):
        pb, pr = pbanks[kq % len(pbanks)]

        def tr(e, kq=kq, pb=pb):
            ins = None
            for a in range(4):
                k = kq * 4 + a
                ins = e.transpose(out=pb[:, a * 128:(a + 1) * 128], in_=xs[:, k * 128:(k + 1) * 128],
                                  identity=C.ident_f[:])
            return ins
        S.op("pe", tr, reads=[C.r_xs], writes=[pr])
        for a in range(4):
            k = kq * 4 + a
            kk = k_list[k]
            if kq % 2 == 0:
                S.op("act", lambda e, a=a, kk=kk, k=k, pb=pb: e.activation(
                    out=dstT[:, kk, col0:col0 + 128], in_=pb[:, a * 128:(a + 1) * 128], func=AF.Copy,
                    scale=gcols[:, k:k + 1]), reads=[pr] + ([g_res] if g_res else []), pwrites=[dst_res])
            else:
                S.op("dve", lambda e, a=a, kk=kk, k=k, pb=pb: e.tensor_scalar(
                    out=dstT[:, kk, col0:col0 + 128], in0=pb[:, a * 128:(a + 1) * 128],
                    scalar1=gcols[:, k:k + 1], scalar2=None, op0=ALU.mult), reads=[pr] + ([g_res] if g_res else []), pwrites=[dst_res])


def emit_ffn(C, nc, src_d, dst_d, gain_d, Wg, Wu, Wd, T, final=None):
    S = C.S
    NG = T // 512
    with ExitStack() as st:
        xt = [_tile(st, nc, "f_xt%d" % i, [128, D], F32) for i in range(2)]
        r_xt = [Res() for _ in range(2)]
        C.n_xs = _tile(st, nc, "f_xs", [128, D], F32); C.r_xs = Res()
        C.n_junk = _tile(st, nc, "f_junk", [128, D], BF16); C.r_junk = Res()
        C.n_ss = _tile(st, nc, "f_ss", [128, 1], F32); C.r_ss = Res()
        C.n_rs = _tile(st, nc, "f_rs", [128, 1], F32); C.r_rs = Res()
        gcols = _tile(st, nc, "f_gc", [128, KD], F32); r_gc = Res()
        xnT = _tile(st, nc, "f_xnT", [128, KD, 512], BF16); r_xnT = Res()
        hT = _tile(st, nc, "f_hT", [128, NFF, 512], BF16); r_hT = Res()
        wg = [_tile(st, nc, "f_wg%d" % i, [128, KD, 256], BF16) for i in range(2)]
        wu = [_tile(st, nc, "f_wu%d" % i, [128, KD, 256], BF16) for i in range(2)]
        r_wg = [Res() for _ in range(2)]; r_wu = [Res() for _ in range(2)]
        wd = [_tile(st, nc, "f_wd%d" % i, [128, 4, 1024], BF16) for i in range(2)]
        r_wd = [Res() for _ in range(2)]
        sg = [_tile(st, nc, "f_sg%d" % i, [128, 512], F32) for i in range(2)]
        r_sg = [Res() for _ in range(2)]
        xh = [_tile(st, nc, "f_xh%d" % i, [128, 1024], F32) for i in range(2)]
        r_xh = [Res() for _ in range(2)]
        ho = [_tile(st, nc, "f_ho%d" % i, [128, 1024], F32) for i in range(2)]
        r_ho = [Res() for _ in range(2)]
        pb = [_psum(st, nc, "f_pb%d" % i, [128, 512], F32) for i in range(8)]
        r_pb = [Res(excl=True) for _ in range(8)]
        r_dst = Res()

        S.op("sp", lambda e: e.dma_start(out=gcols[:], in_=gain_d),
             writes=[r_gc], dma=True)
        nxt = 0
        for g in range(NG):
            t0 = g * 512
            for i in range(4):
                b = nxt % 2; nxt += 1
                r0 = t0 + i * 128
                S.op("sp", lambda e, b=b, r0=r0: e.dma_start(out=xt[b][:], in_=src_d[r0:r0 + 128, :]),
                     writes=[r_xt[b]], dma=True)
                emit_norm_T(C, xt[b], r_xt[b], gcols, list(range(KD)), xnT, r_xnT, i * 128,
                            [(pb[4], r_pb[4]), (pb[5], r_pb[5])], g_res=r_gc)
            for s in range(NFF // 2):
                b = s % 2
                c0 = s * 256
                S.op("pool", lambda e, b=b, c0=c0: e.dma_start(
                    out=wg[b][:], in_=Wg[:, c0:c0 + 256].rearrange("(k p) c -> p k c", p=128)),
                    writes=[r_wg[b]], dma=True)
                S.op("pool", lambda e, b=b, c0=c0: e.dma_start(
                    out=wu[b][:], in_=Wu[:, c0:c0 + 256].rearrange("(k p) c -> p k c", p=128)),
                    writes=[r_wu[b]], dma=True)
                for jj in range(2):
                    j = s * 2 + jj
                    pg, pu = pb[(j % 2) * 2], pb[(j % 2) * 2 + 1]
                    rg, ru = r_pb[(j % 2) * 2], r_pb[(j % 2) * 2 + 1]

                    def mm(e, w=wg[b], p=pg, jj=jj):
                        ins = None
                        for k in range(KD):
                            ins = e.matmul(p[:], lhsT=w[:, k, jj * 128:(jj + 1) * 128], rhs=xnT[:, k, :],
                                           start=(k == 0), stop=(k == KD - 1))
                        return ins
                    S.op("pe", mm, reads=[r_wg[b], r_xnT], writes=[rg])
                    S.op("pe", lambda e, w=wu[b], p=pu, jj=jj: mm(e, w, p, jj), reads=[r_wu[b], r_xnT], writes=[ru])
                    sb = j % 2
                    S.op("act", lambda e, sb=sb, pg=pg: e.activation(out=sg[sb][:], in_=pg[:], func=AF.Silu),
                         reads=[rg], writes=[r_sg[sb]])
                    S.op("dve", lambda e, sb=sb, pu=pu, j=j: e.tensor_tensor(
                        out=hT[:, j, :], in0=sg[sb][:], in1=pu[:], op=ALU.mult),
                        reads=[r_sg[sb], ru], pwrites=[r_hT])
            for hf in range(2):
                for j0 in range(0, NFF, 4):
                    b = (j0 // 4) % 2
                    S.op("pool", lambda e, b=b, j0=j0, hf=hf: e.dma_start(
                        out=wd[b][:], in_=Wd[j0 * 128:(j0 + 4) * 128, hf * 1024:(hf + 1) * 1024].rearrange(
                            "(j p) c -> p j c", p=128)), writes=[r_wd[b]], dma=True)
                    for jj in range(4):
                        j = j0 + jj

                        def dn(e, b=b, jj=jj, j=j):
                            ins = None
                            for i in range(4):
                                for dd in range(2):
                                    ins = e.matmul(pb[i * 2 + dd][:], lhsT=hT[:, j, i * 128:(i + 1) * 128],
                                                   rhs=wd[b][:, jj, dd * 512:(dd + 1) * 512],
                                                   start=(j == 0), stop=(j == NFF - 1))
                            return ins
                        if j == 0:
                            S.op("pe", dn, reads=[r_wd[b], r_hT], writes=r_pb)
                        else:
                            S.op("pe", dn, reads=[r_wd[b], r_hT], pwrites=r_pb)
                for i in range(4):
                    b = nxt % 2; nxt += 1
                    r0 = t0 + i * 128
                    S.op("sp", lambda e, b=b, r0=r0, hf=hf: e.dma_start(
                        out=xh[b][:], in_=src_d[r0:r0 + 128, hf * 1024:(hf + 1) * 1024]),
                        writes=[r_xh[b]], dma=True)
                    for dd in range(2):
                        S.op("dve", lambda e, b=b, i=i, dd=dd: e.scalar_tensor_tensor(
                            out=ho[b][:, dd * 512:(dd + 1) * 512], in0=pb[i * 2 + dd][:], scalar=0.5,
                            in1=xh[b][:, dd * 512:(dd + 1) * 512], op0=ALU.mult, op1=ALU.add),
                            reads=[r_pb[i * 2 + dd], r_xh[b]], pwrites=[r_ho[b]])
                    S.op("sp", lambda e, b=b, r0=r0, hf=hf: e.dma_start(
                        out=dst_d[r0:r0 + 128, hf * 1024:(hf + 1) * 1024], in_=ho[b][:]),
                        reads=[r_ho[b]], pwrites=[r_dst], dma=True)
        S.barrier()
        S.flush()
    return r_dst


def region_sizes(TP, TS):
    A, NA, B = {}, {}, {}
    for part, Tp in (("P", TP), ("S", TS)):
        A["qkv" + part] = (0, 8 * 3 * 128 * Tp)
        A["ab" + part] = (8 * 3 * 128 * Tp, 32 * Tp)
        NA[part] = (8 * 3 * 128 + 32) * Tp
    off = 0
    for name, n in (("szP", 8 * 128 * TP), ("szS", 8 * 128 * TS), ("kTP", 2 * 128 * TP), ("kTS", 2 * 128 * TS),
                    ("vP", TP * 256), ("vS", TS * 256)):
        B[name] = (off, n); off += n
    NB = off
    return A, NA, B, NB


def flat(ap2d):
    return ap2d.rearrange("a b -> (a b)")


def emit_proj(C, nc, h1_d, gain_d, w_in, XA_x, XB_x, qT_d, rope_d, vecs_d, T, TP, TS):
    S = C.S
    NG = T // 512
    RA, NA, RB, NB = region_sizes(TP, TS)
    fa, fb = {p: flat(XA_x[p]) for p in XA_x}, flat(XB_x)

    def regA(name, pattern, **kw):
        o, n = RA[name]
        return fa[name[-1]][o:o + n].rearrange(pattern, **kw)

    def regB(name, pattern, **kw):
        o, n = RB[name]
        return fb[o:o + n].rearrange(pattern, **kw)
    qkv_v = {"P": regA("qkvP", "(h j f t) -> h j f t", h=8, j=3, f=128), "S": regA("qkvS", "(h j f t) -> h j f t", h=8, j=3, f=128)}
    ab_v = {"P": regA("abP", "(r t) -> r t", r=32), "S": regA("abS", "(r t) -> r t", r=32)}
    sz_v = {"P": regB("szP", "(h f t) -> h f t", h=8, f=128), "S": regB("szS", "(h f t) -> h f t", h=8, f=128)}
    kT_v = {"P": regB("kTP", "(h f t) -> f h t", h=2, f=128), "S": regB("kTS", "(h f t) -> f h t", h=2, f=128)}
    v_v = {"P": regB("vP", "(t c) -> t c", c=256), "S": regB("vS", "(t c) -> t c", c=256)}
    qT_v = qT_d.rearrange("h f t -> f h t")
    with ExitStack() as st:
        xt = [_tile(st, nc, "p_xt%d" % i, [128, D], F32) for i in range(2)]
        r_xt = [Res() for _ in range(2)]
        C.n_xs = _tile(st, nc, "p_xs", [128, D], F32); C.r_xs = Res()
        C.n_junk = _tile(st, nc, "p_junk", [128, D], BF16); C.r_junk = Res()
        C.n_ss = _tile(st, nc, "p_ss", [128, 1], F32); C.r_ss = Res()
        C.n_rs = _tile(st, nc, "p_rs", [128, 1], F32); C.r_rs = Res()
        gcols = _tile(st, nc, "p_gc", [128, KD], F32); r_gc = Res()
        uT = _tile(st, nc, "p_uT", [128, KD, 512], BF16); r_uT = Res()
        ws = [_tile(st, nc, "p_w%d" % i, [128, KD, 256], BF16) for i in range(3)]
        r_ws = [Res() for _ in range(3)]
        wab = _tile(st, nc, "p_wab", [128, KD, 32], BF16); r_wab = Res()
        stg = [_tile(st, nc, "p_stg%d" % i, [128, 512], F32) for i in range(3)]
        r_stg = [Res() for _ in range(3)]
        stgb = [_tile(st, nc, "p_stgb%d" % i, [128, 512], BF16) for i in range(2)]
        r_stgb = [Res() for _ in range(2)]
        gbc = _tile(st, nc, "p_gbc", [128, 2, 128], F32); r_gbc = Res()
        rope = [_tile(st, nc, "p_rope%d" % i, [128, 2, 2, 32], F32) for i in range(4)]
        r_rope = [Res() for _ in range(4)]
        qs = _tile(st, nc, "p_qs", [128, 2, 128], F32); r_qs = Res()
        sq = _tile(st, nc, "p_sq", [128, 2, 128], F32); r_sq = Res()
        ss2 = _tile(st, nc, "p_ss2", [128, 2], F32); r_ss2 = Res()
        qn = _tile(st, nc, "p_qn", [128, 2, 2, 2, 32], F32); r_qn = Res()
        ta = _tile(st, nc, "p_ta", [128, 2, 32], F32); r_ta = Res()
        tb = _tile(st, nc, "p_tb", [128, 2, 32], F32); r_tb = Res()
        qr = [_tile(st, nc, "p_qr%d" % i, [128, 2, 2, 2, 32], BF16) for i in range(2)]
        r_qr = [Res() for _ in range(2)]
        vst = [_tile(st, nc, "p_vst%d" % i, [128, 256], BF16) for i in range(2)]
        r_vst = [Res() for _ in range(2)]
        QTs = _tile(st, nc, "p_QTs", [128, 8, 512], BF16); r_QTs = Res()
        KTs = _tile(st, nc, "p_KTs", [128, 2, 512], BF16); r_KTs = Res()
        identb = C.ident_b
        pT = [_psum(st, nc, "p_pT%d" % i, [128, 512], F32) for i in range(2)]
        pf = [_psum(st, nc, "p_pf%d" % i, [128, 512], F32) for i in range(2)]
        pa = [_psum(st, nc, "p_pa%d" % i, [128, 512], F32) for i in range(2)]
        ptr = [_psum(st, nc, "p_ptr%d" % i, [128, 1024], BF16) for i in range(2)]
        r_pT = [Res(excl=True) for _ in range(2)]; r_pf = [Res(excl=True) for _ in range(2)]
        r_pa = [Res(excl=True) for _ in range(2)]; r_ptr = [Res(excl=True) for _ in range(2)]
        r_out = C.r_xch

        S.op("sp", lambda e: e.dma_start(out=gcols[:], in_=gain_d), writes=[r_gc], dma=True)
        for i in range(2):
            S.op("sp", lambda e, i=i: e.dma_start(out=gbc[:, i, :], in_=vecs_d[i].partition_broadcast(128)),
                 pwrites=[r_gbc], dma=True)
        nxt = 0
        wrr = 0
        cnt_pa = 0
        for g in range(NG):
            t0 = g * 512
            part = "P" if t0 < TP else "S"
            o = t0 if part == "P" else t0 - TP
            for i in range(4):
                b = nxt % 2; nxt += 1
                r0 = t0 + i * 128
                S.op("sp", lambda e, b=b, r0=r0: e.dma_start(out=xt[b][:], in_=h1_d[r0:r0 + 128, :]),
                     writes=[r_xt[b]], dma=True)
                emit_norm_T(C, xt[b], r_xt[b], gcols, list(range(KD)), uT, r_uT, i * 128,
                            [(pT[0], r_pT[0]), (pT[1], r_pT[1])], g_res=r_gc)
                S.op("sp", lambda e, i=i, r0=r0: e.dma_start(
                    out=rope[i][:].rearrange("p a b c -> p (a b c)"), in_=rope_d[r0:r0 + 128, :]),
                    writes=[r_rope[i]], dma=True)
            for cs in range(16):
                wb = wrr % 3; wrr += 1
                S.op("pool", lambda e, wb=wb, cs=cs: e.dma_start(
                    out=ws[wb][:], in_=w_in[:, cs * 256:(cs + 1) * 256].rearrange("(k p) c -> p k c", p=128)),
                    writes=[r_ws[wb]], dma=True)
                for jj in range(2):
                    cc = cs * 2 + jj
                    j, h = cc // 8, cc % 8
                    p, rp = pf[cc % 2], r_pf[cc % 2]

                    def mm(e, wb=wb, jj=jj, p=p):
                        ins = None
                        for k in range(KD):
                            ins = e.matmul(p[:], lhsT=ws[wb][:, k, jj * 128:(jj + 1) * 128], rhs=uT[:, k, :],
                                           start=(k == 0), stop=(k == KD - 1))
                        return ins
                    S.op("pe", mm, reads=[r_ws[wb], r_uT], writes=[rp])
                    if j < 3:
                        sb = cc % 3
                        if cc % 2 == 0:
                            S.op("act", lambda e, sb=sb, p=p: e.activation(out=stg[sb][:], in_=p[:], func=AF.Copy),
                                 reads=[rp], writes=[r_stg[sb]])
                        else:
                            S.op("dve", lambda e, sb=sb, p=p: e.tensor_copy(out=stg[sb][:], in_=p[:]),
                                 reads=[rp], writes=[r_stg[sb]])
                        S.op("sp", lambda e, sb=sb, h=h, j=j, part=part, o=o: e.dma_start(
                            out=qkv_v[part][h, j, :, o:o + 512], in_=stg[sb][:]),
                            reads=[r_stg[sb]], pwrites=[r_out], dma=True)
                    else:
                        sb = cc % 2
                        S.op("act", lambda e, sb=sb, p=p: e.activation(out=stgb[sb][:], in_=p[:], func=AF.Silu),
                             reads=[rp], writes=[r_stgb[sb]])
                        S.op("sp", lambda e, sb=sb, h=h, part=part, o=o: e.dma_start(
                            out=sz_v[part][h, :, o:o + 512], in_=stgb[sb][:]),
                            reads=[r_stgb[sb]], pwrites=[r_out], dma=True)
            S.op("pool", lambda e: e.dma_start(
                out=wab[:], in_=w_in[:, 4096:4128].rearrange("(k p) c -> p k c", p=128)),
                writes=[r_wab], dma=True)

            def mmab(e):
                ins = None
                for k in range(KD):
                    ins = e.matmul(pf[0][0:32, :], lhsT=wab[:, k, 0:32], rhs=uT[:, k, :],
                                   start=(k == 0), stop=(k == KD - 1))
                return ins
            S.op("pe", mmab, reads=[r_wab, r_uT], writes=[r_pf[0]])
            S.op("dve", lambda e: e.tensor_copy(out=stg[0][0:32, :], in_=pf[0][0:32, :]),
                 reads=[r_pf[0]], writes=[r_stg[0]])
            S.op("sp", lambda e, part=part, o=o: e.dma_start(out=ab_v[part][:, o:o + 512], in_=stg[0][0:32, :]),
                 reads=[r_stg[0]], pwrites=[r_out], dma=True)
            for s6 in range(6):
                wb = wrr % 3; wrr += 1
                c0 = 4128 + s6 * 256
                S.op("pool", lambda e, wb=wb, c0=c0: e.dma_start(
                    out=ws[wb][:], in_=w_in[:, c0:c0 + 256].rearrange("(k p) c -> p k c", p=128)),
                    writes=[r_ws[wb]], dma=True)
                for i in range(4):
                    pi = cnt_pa % 2; cnt_pa += 1
                    p, rp = pa[pi], r_pa[pi]

                    def mma(e, wb=wb, i=i, p=p):
                        ins = None
                        for k in range(KD):
                            ins = e.matmul(p[:, 0:256], lhsT=uT[:, k, i * 128:(i + 1) * 128], rhs=ws[wb][:, k, :],
                                           start=(k == 0), stop=(k == KD - 1))
                        return ins
                    S.op("pe", mma, reads=[r_ws[wb], r_uT], writes=[rp])
                    if s6 == 5:
                        vb = i % 2
                        S.op("act", lambda e, vb=vb, p=p: e.activation(out=vst[vb][:], in_=p[:, 0:256], func=AF.Copy),
                             reads=[rp], writes=[r_vst[vb]])
                        S.op("sp", lambda e, vb=vb, part=part, o=o, i=i: e.dma_start(
                            out=v_v[part][o + i * 128:o + (i + 1) * 128, :], in_=vst[vb][:]),
                            reads=[r_vst[vb]], pwrites=[r_out], dma=True)
                        continue
                    gi = 0 if s6 < 4 else 1
                    S.op("act", lambda e, p=p: e.activation(out=qs[:].rearrange("p a b -> p (a b)"), in_=p[:, 0:256],
                                                            func=AF.Copy), reads=[rp], writes=[r_qs])
                    S.op("dve", lambda e: e.tensor_tensor(out=sq[:], in0=qs[:], in1=qs[:], op=ALU.mult),
                         reads=[r_qs], writes=[r_sq])
                    S.op("dve", lambda e: e.tensor_reduce(out=ss2[:], in_=sq[:], axis=AX.X, op=ALU.add),
                         reads=[r_sq], writes=[r_ss2])
                    S.op("dve", lambda e: e.tensor_scalar(out=ss2[:], in0=ss2[:], scalar1=1.0 / 128, scalar2=EPS,
                                                          op0=ALU.mult, op1=ALU.add), reads=[r_ss2], writes=[r_ss2])
                    S.op("act", lambda e: e.activation(out=ss2[:], in_=ss2[:], func=AF.Sqrt),
                         reads=[r_ss2], writes=[r_ss2])
                    S.op("dve", lambda e: e.reciprocal(out=ss2[:], in_=ss2[:]), reads=[r_ss2], writes=[r_ss2])
                    qb = cnt_pa % 2
                    for hh in range(2):
                        S.op("dve", lambda e, hh=hh, gi=gi: e.scalar_tensor_tensor(
                            out=qn[:, hh].rearrange("p a b c -> p (a b c)"), in0=qs[:, hh, :], scalar=ss2[:, hh:hh + 1],
                            in1=gbc[:, gi, :], op0=ALU.mult, op1=ALU.mult),
                            reads=[r_qs, r_ss2, r_gbc], pwrites=[r_qn])
                    for hh in range(2):
                        x1 = qn[:, hh, :, 0, :]
                        x2 = qn[:, hh, :, 1, :]
                        Cc = rope[i][:, 0]
                        Sn = rope[i][:, 1]
                        S.op("dve", lambda e, x1=x1, Cc=Cc: e.tensor_tensor(out=ta[:], in0=x1, in1=Cc, op=ALU.mult),
                             reads=[r_qn, r_rope[i]], writes=[r_ta])
                        S.op("dve", lambda e, x2=x2, Sn=Sn: e.tensor_tensor(out=tb[:], in0=x2, in1=Sn, op=ALU.mult),
                             reads=[r_qn, r_rope[i]], writes=[r_tb])
                        S.op("dve", lambda e, hh=hh, qb=qb: e.tensor_tensor(out=qr[qb][:, hh, :, 0, :], in0=ta[:], in1=tb[:],
                                                                            op=ALU.subtract),
                             reads=[r_ta, r_tb], pwrites=[r_qr[qb]])
                        S.op("dve", lambda e, x2=x2, Cc=Cc: e.tensor_tensor(out=ta[:], in0=x2, in1=Cc, op=ALU.mult),
                             reads=[r_qn, r_rope[i]], writes=[r_ta])
                        S.op("dve", lambda e, x1=x1, Sn=Sn: e.tensor_tensor(out=tb[:], in0=x1, in1=Sn, op=ALU.mult),
                             reads=[r_qn, r_rope[i]], writes=[r_tb])
                        S.op("dve", lambda e, hh=hh, qb=qb: e.tensor_tensor(out=qr[qb][:, hh, :, 1, :], in0=ta[:], in1=tb[:],
                                                                            op=ALU.add),
                             reads=[r_ta, r_tb], pwrites=[r_qr[qb]])
                    pt, rpt = ptr[qb], r_ptr[qb]

                    def trq(e, qb=qb, pt=pt):
                        ins = None
                        for hh in range(2):
                            ins = e.transpose(out=pt[:, hh * 128:(hh + 1) * 128],
                                              in_=qr[qb][:, hh].rearrange("p a b c -> p (a b c)"), identity=identb[:])
                        return ins
                    S.op("pe", trq, reads=[r_qr[qb]], writes=[rpt])
                    if s6 < 4:
                        S.op("act", lambda e, pt=pt, s6=s6, i=i: e.activation(
                            out=QTs[:, 2 * s6:2 * s6 + 2, i * 128:(i + 1) * 128],
                            in_=pt[:, 0:256].rearrange("p (a t) -> p a t", a=2), func=AF.Copy),
                            reads=[rpt], pwrites=[r_QTs])
                    else:
                        S.op("act", lambda e, pt=pt, i=i: e.activation(
                            out=KTs[:, :, i * 128:(i + 1) * 128],
                            in_=pt[:, 0:256].rearrange("p (a t) -> p a t", a=2), func=AF.Copy),
                            reads=[rpt], pwrites=[r_KTs])
                    r_qr[qb].r = dict(r_qr[qb].r)
            S.op("sp", lambda e, t0=t0: e.dma_start(out=qT_v[:, :, t0:t0 + 512], in_=QTs[:]),
                 reads=[r_QTs], pwrites=[C.r_qT], dma=True)
            S.op("sp", lambda e, part=part, o=o: e.dma_start(out=kT_v[part][:, :, o:o + 512], in_=KTs[:]),
                 reads=[r_KTs], pwrites=[r_out], dma=True)
        S.barrier()
        S.flush()


def emit_attn(C, nc, XB_g, qT_d, mixT_d, vecs_d, T, TP, TS):
    S = C.S
    RA, NA, RB, NB = region_sizes(TP, TS)
    fbg = flat(XB_g)
    SCALE = float(HD) ** -0.5
    LqM = max(TP, TS)
    with ExitStack() as st:
        KT = _tile(st, nc, "a_KT", [128, 8 * LqM], BF16); r_KT = Res()
        V1 = _tile(st, nc, "a_V1", [128, 8 * LqM // 128, 130], BF16); r_V1 = Res()
        QB = [_tile(st, nc, "a_QB%d" % i, [128, 4, 128], BF16) for i in range(2)]
        r_QB = [Res() for _ in range(2)]
        PT = [_tile(st, nc, "a_PT%d" % i, [128, 512], BF16) for i in range(3)]
        r_PT = [Res() for _ in range(3)]
        oT = _tile(st, nc, "a_oT", [128, 4, LqM], BF16); r_oT = Res()
        ot = _tile(st, nc, "a_ot", [128, 4, 128], F32); r_ot = Res()
        sq = _tile(st, nc, "a_sq", [128, 4, 128], F32); r_sq = Res()
        onb = _tile(st, nc, "a_onb", [128, 4, 128], BF16); r_onb = Res()
        rec = _tile(st, nc, "a_rec", [128, 4], F32); r_rec = Res()
        ss4 = _tile(st, nc, "a_ss4", [128, 4], F32); r_ss4 = Res()
        gbc = _tile(st, nc, "a_gbc", [128, 128], F32); r_gbc = Res()
        vq = _tile(st, nc, "a_vq", [1, 2, 128], F32); r_vq = Res()
        m2 = _tile(st, nc, "a_m2", [1, 2], F32); r_m2 = Res()
        ones_r = _tile(st, nc, "a_ones", [1, 128], F32); r_ones = Res()
        negM = _tile(st, nc, "a_negM", [128, 1], F32); r_negM = Res()
        ps = [_psum(st, nc, "a_ps%d" % i, [128, 512], F32) for i in range(3)]
        po = [_psum(st, nc, "a_po%d" % i, [128, 512], F32) for i in range(4)]
        ptr = _psum(st, nc, "a_ptr", [128, 1024], BF16)
        r_ps = [Res(excl=True) for _ in range(3)]; r_po = [Res(excl=True) for _ in range(4)]
        r_ptr = Res(excl=True)

        S.op("sp", lambda e: e.dma_start(out=vq[:], in_=vecs_d[0:2, :].rearrange("(o a) d -> o a d", o=1)),
             writes=[r_vq], dma=True)
        S.op("sp", lambda e: e.dma_start(out=gbc[:], in_=vecs_d[2].partition_broadcast(128)), writes=[r_gbc], dma=True)
        S.op("dve", lambda e: e.memset(ones_r[:], 1.0), writes=[r_ones])
        S.op("dve", lambda e: e.memset(V1[:], 1.0), writes=[r_V1])
        S.op("dve", lambda e: e.tensor_reduce(out=m2[:], in_=vq[:], axis=AX.X, op=ALU.max, apply_absolute_value=True),
             reads=[r_vq], writes=[r_m2])
        S.op("dve", lambda e: e.tensor_tensor(out=m2[:, 0:1], in0=m2[:, 0:1], in1=m2[:, 1:2], op=ALU.mult),
             reads=[r_m2], writes=[r_m2])
        S.op("dve", lambda e: e.tensor_scalar(out=m2[:, 0:1], in0=m2[:, 0:1], scalar1=-(float(HD) ** 0.5), scalar2=None,
                                              op0=ALU.mult), reads=[r_m2], writes=[r_m2])
        S.op("pe", lambda e: e.matmul(ps[0][:, 0:1], lhsT=ones_r[:], rhs=m2[:, 0:1], start=True, stop=True),
             reads=[r_ones, r_m2], writes=[r_ps[0]])
        S.op("dve", lambda e: e.tensor_copy(out=negM[:], in_=ps[0][:, 0:1]), reads=[r_ps[0]], writes=[r_negM])

        cnt = 0
        qcnt = 0
        for part, Lq, lt0 in (("P", TP, 0), ("S", TS, TP)):
            nkb = 8 * Lq // 128
            okT, _ = RB["kT" + part]
            ov, _ = RB["v" + part]
            for kvh in range(2):
                for r in range(NCORE):
                    kreg = fbg[r * NB + okT:r * NB + okT + 2 * 128 * Lq].rearrange("(h f t) -> h f t", h=2, f=128)
                    vreg = fbg[r * NB + ov:r * NB + ov + Lq * 256].rearrange("(t c) -> t c", c=256)
                    S.op("sp", lambda e, r=r, kreg=kreg, kvh=kvh, Lq=Lq: e.dma_start(
                        out=KT[:, r * Lq:(r + 1) * Lq], in_=kreg[kvh]), reads=[C.r_xchg], pwrites=[r_KT], dma=True)
                    nb = Lq // 128
                    S.op("sp", lambda e, r=r, vreg=vreg, kvh=kvh, nb=nb: e.dma_start(
                        out=V1[:, r * nb:(r + 1) * nb, 0:128],
                        in_=vreg[:, kvh * 128:(kvh + 1) * 128].rearrange("(b p) d -> p b d", p=128)),
                        reads=[C.r_xchg], pwrites=[r_V1], dma=True)
                for qb in range(Lq // 128):
                    q0 = lt0 + qb * 128
                    qi = qcnt % 2; qcnt += 1
                    S.op("sp", lambda e, qi=qi, kvh=kvh, q0=q0: e.dma_start(
                        out=QB[qi][:], in_=qT_d[4 * kvh:4 * kvh + 4, :, q0:q0 + 128].rearrange("h f t -> f h t")),
                        reads=[C.r_qT], writes=[r_QB[qi]], dma=True)
                    LAG = 2
                    bufs = []
                    for kk in range(nkb + LAG):
                        if kk < nkb:
                            kb = kk
                            b = cnt % 3; cnt += 1
                            bufs.append(b)
                            S.op("pe", lambda e, b=b, kb=kb, qi=qi: e.matmul(
                                ps[b][:], lhsT=KT[:, kb * 128:(kb + 1) * 128], rhs=QB[qi][:].rearrange("p a t -> p (a t)"),
                                start=True, stop=True), reads=[r_KT, r_QB[qi]], writes=[r_ps[b]])
                            S.op("act", lambda e, b=b: e.activation(out=PT[b][:], in_=ps[b][:], func=AF.Exp,
                                                                    bias=negM[:, 0:1], scale=SCALE),
                                 reads=[r_ps[b], r_negM], writes=[r_PT[b]])
                        if kk >= LAG:
                            kb = kk - LAG
                            b = bufs[kb]

                            def pv(e, b=b, kb=kb, nkb=nkb):
                                ins = None
                                for hh in range(4):
                                    ins = e.matmul(po[hh][:, 0:129], lhsT=PT[b][:, hh * 128:(hh + 1) * 128],
                                                   rhs=V1[:, kb, 0:129], start=(kb == 0), stop=(kb == nkb - 1))
                                return ins
                            if kb == 0:
                                S.op("pe", pv, reads=[r_PT[b], r_V1], writes=r_po)
                            else:
                                S.op("pe", pv, reads=[r_PT[b], r_V1], pwrites=r_po)
                    for hh in range(4):
                        S.op("dve", lambda e, hh=hh: e.reciprocal(out=rec[:, hh:hh + 1], in_=po[hh][:, 128:129]),
                             reads=[r_po[hh]], pwrites=[r_rec])
                        S.op("dve", lambda e, hh=hh: e.tensor_scalar(out=ot[:, hh, :], in0=po[hh][:, 0:128],
                                                                     scalar1=rec[:, hh:hh + 1], scalar2=None, op0=ALU.mult),
                             reads=[r_po[hh], r_rec], pwrites=[r_ot])
                    S.op("dve", lambda e: e.tensor_tensor(out=sq[:], in0=ot[:], in1=ot[:], op=ALU.mult),
                         reads=[r_ot], writes=[r_sq])
                    S.op("dve", lambda e: e.tensor_reduce(out=ss4[:], in_=sq[:], axis=AX.X, op=ALU.add),
                         reads=[r_sq], writes=[r_ss4])
                    S.op("dve", lambda e: e.tensor_scalar(out=ss4[:], in0=ss4[:], scalar1=1.0 / 128, scalar2=EPS,
                                                          op0=ALU.mult, op1=ALU.add), reads=[r_ss4], writes=[r_ss4])
                    S.op("act", lambda e: e.activation(out=ss4[:], in_=ss4[:], func=AF.Sqrt), reads=[r_ss4], writes=[r_ss4])
                    S.op("dve", lambda e: e.reciprocal(out=ss4[:], in_=ss4[:]), reads=[r_ss4], writes=[r_ss4])
                    for hh in range(4):
                        S.op("dve", lambda e, hh=hh: e.scalar_tensor_tensor(
                            out=onb[:, hh, :], in0=ot[:, hh, :], scalar=ss4[:, hh:hh + 1], in1=gbc[:],
                            op0=ALU.mult, op1=ALU.mult), reads=[r_ot, r_ss4, r_gbc], pwrites=[r_onb])

                    def trs(e):
                        ins = None
                        for hh in range(4):
                            ins = e.transpose(out=ptr[:, hh * 128:(hh + 1) * 128], in_=onb[:, hh, :], identity=C.ident_b[:])
                        return ins
                    S.op("pe", trs, reads=[r_onb], writes=[r_ptr])
                    S.op("act", lambda e, qb=qb: e.activation(out=oT[:, :, qb * 128:(qb + 1) * 128],
                                                              in_=ptr[:, 0:512].rearrange("p (a t) -> p a t", a=4), func=AF.Copy),
                         reads=[r_ptr], pwrites=[r_oT])
                S.op("sp", lambda e, kvh=kvh, lt0=lt0, Lq=Lq: e.dma_start(
                    out=mixT_d[4 * kvh:4 * kvh + 4, :, lt0:lt0 + Lq].rearrange("h f t -> f h t"), in_=oT[:, :, 0:Lq]),
                    reads=[r_oT], pwrites=[C.r_mixA], dma=True)
        S.barrier()
        S.flush()


def idx_layout(LP, LS):
    keys = []
    for part in ("P", "S"):
        for r in range(NCORE):
            for j in range(3):
                keys += [("q", part, r, j, "m"), ("q", part, r, j, "l"), ("q", part, r, j, "r")]
            keys += [("sz", part, r), ("a", part, r), ("b", part, r)]
    TP, TS = LP // NCORE, LS // NCORE
    for g in range((TP + TS) // 512):
        for h in range(8):
            keys.append(("og", g, h))
    return {k: i for i, k in enumerate(keys)}


def og_sizes(LP, LS):
    return {"P": (0, 128 * LP), "S": (128 * LP, 128 * LS)}, 128 * (LP + LS)


def emit_gdn(C, nc, XA_g, XB_g, OG_x, of_d, idx_t, r_idx, hv_d, vecs_d, convw_d, masks_d, LP, LS):
    S = C.S
    NB = 4
    TP, TS = LP // NCORE, LS // NCORE
    RA, NA, RB, NB_ = region_sizes(TP, TS)
    fagd, fbg = {p: flat(XA_g[p]) for p in XA_g}, flat(XB_g)
    KI = idx_layout(LP, LS)
    OGR, NO = og_sizes(LP, LS)
    fog = flat(OG_x)
    SEGM = max(TP, TS)
    W = NB * 128
    with ExitStack() as st:
        T_ = lambda name, shape, dt=F32: _tile(st, nc, "g_" + name, shape, dt)
        raw = [T_("raw%d" % j, [128, SEGM + 4]) for j in range(3)]; r_raw = [Res() for _ in range(3)]
        cv = [T_("cv%d" % j, [128, SEGM]) for j in range(3)]; r_cv = [Res() for _ in range(3)]
        sqs = T_("sqs", [128, 512]); r_sqs = Res()
        rn = T_("rn", [128, 512]); r_rn = Res()
        qTb = T_("qTb", [128, SEGM], BF16); kTb = T_("kTb", [128, SEGM], BF16)
        szs = [T_("szs%d" % i, [128, SEGM], BF16) for i in range(2)]; r_szs = [Res() for _ in range(2)]
        ogs = [T_("ogs%d" % i, [128, SEGM], BF16) for i in range(2)]; r_ogs = [Res() for _ in range(2)]
        a2 = T_("a2", [2, SEGM]); b2 = T_("b2", [2, SEGM]); r_a2 = Res(); r_b2 = Res()
        t2a = T_("t2a", [2, SEGM]); t2b = T_("t2b", [2, SEGM]); r_t2a = Res(); r_t2b = Res()
        hv = T_("hv", [2, 4]); r_hv = Res()
        cw = T_("cw", [128, 3, 5]); r_cw = Res()
        gbc = T_("gbc", [128, 128]); r_gbc = Res()
        masks = T_("masks", [128, 7, 128]); r_masks = Res()
        ones_f = masks[:, 4, :]
        identf = C.ident_f
        gbt = T_("gbt", [128, NB, 4]); r_gbt = Res()
        g_bc = T_("g_bc", [128, NB, 128]); r_g_bc = Res()
        cols = T_("cols", [128, NB, 8]); r_cols = Res()
        Dm = T_("Dm", [128, NB, 128]); r_Dm = Res()
        decI = T_("decI", [128, NB, 128]); decS = T_("decS", [128, NB, 128]); r_decI = Res(); r_decS = Res()
        dgb = T_("dgb", [128, NB, 128]); r_dgb = Res()
        XA_ = [T_("XA%d" % i, [128, NB, 128]) for i in range(2)]; r_XA = [Res() for _ in range(2)]
        XT_ = [T_("XT%d" % i, [128, NB, 128]) for i in range(2)]; r_XT = [Res() for _ in range(2)]
        Pm = T_("Pm", [128, NB, 128]); r_Pm = Res()
        Pb = T_("Pb", [128, NB, 128], BF16); r_Pb = Res()
        Egc = T_("Egc", [128, NB, 128]); r_Egc = Res()
        bv = T_("bv", [128, NB, 128], BF16); bke = T_("bke", [128, NB, 128], BF16)
        r_bv = Res(); r_bke = Res()
        AQT = [T_("AQT%d" % i, [128, NB, 128], BF16) for i in range(2)]; r_AQT = [Res() for _ in range(2)]
        qdT = [T_("qdT%d" % i, [128, NB, 128], BF16) for i in range(2)]; r_qdT = [Res() for _ in range(2)]
        kdc = [T_("kdc%d" % i, [128, NB, 128], BF16) for i in range(2)]; r_kdc = [Res() for _ in range(2)]
        u_t = [T_("u%d" % i, [128, NB, 128]) for i in range(2)]; r_u = [Res() for _ in range(2)]
        wT = [T_("wT%d" % i, [128, NB, 128], BF16) for i in range(2)]; r_wT = [Res() for _ in range(2)]
        cdx = [T_("cdx%d" % i, [128, NB, 2]) for i in range(2)]; r_cdx = [Res() for _ in range(2)]
        vn = T_("vn", [128, 128], BF16); r_vn = Res()
        Sst = T_("S", [128, 128]); r_S = Res()
        Sb = T_("Sb", [128, 128], BF16); r_Sb = Res()
        o_t = [T_("o_t%d" % i, [128, 128]) for i in range(2)]; r_o = [Res() for _ in range(2)]
        of_t = T_("of_t", [128, 128]); r_of = Res()
        osq = T_("osq", [128, 128]); r_osq = Res()
        oss = T_("oss", [128, 1]); r_oss = Res()
        on_t = T_("on_t", [128, 128]); r_on = Res()
        Bk = [_psum(st, nc, "g_B%d" % i, [128, 512], F32) for i in range(5)]
        r_B = [Res(excl=True) for _ in range(5)]
        pS = _psum(st, nc, "g_pS", [128, 512], F32); r_pS = Res(excl=True)
        pD = _psum(st, nc, "g_pD", [128, 512], F32); r_pD = Res(excl=True)
        pN = _psum(st, nc, "g_pN", [128, 512], F32); r_pN = Res(excl=True)

        S.op("sp", lambda e: e.dma_start(out=hv[:], in_=hv_d), writes=[r_hv], dma=True)
        S.op("sp", lambda e: e.dma_start(out=cw[:], in_=convw_d), writes=[r_cw], dma=True)
        S.op("sp", lambda e: e.dma_start(out=gbc[:], in_=vecs_d[3].partition_broadcast(128)), writes=[r_gbc], dma=True)
        S.op("sp", lambda e: e.dma_start(out=masks[:], in_=masks_d), writes=[r_masks], dma=True)
        S.op("act", lambda e: e.activation(out=hv[:, 2:3], in_=hv[:, 0:1], func=AF.Exp), reads=[r_hv], writes=[r_hv])
        S.op("dve", lambda e: e.tensor_scalar(out=hv[:, 2:3], in0=hv[:, 2:3], scalar1=-1.0, scalar2=None, op0=ALU.mult),
             reads=[r_hv], writes=[r_hv])

        def gather(out_ap, src_flat, F, key, npart, reads, wres, partial=True):
            col = KI[key]
            view = src_flat.rearrange("(m f) -> m f", f=F)
            kw = dict(pwrites=[wres]) if partial else dict(writes=[wres])
            S.op("pool", lambda e: e.indirect_dma_start(
                out=out_ap, out_offset=None, in_=view,
                in_offset=bass.IndirectOffsetOnAxis(ap=idx_t[0:npart, col:col + 1], axis=0)),
                reads=reads + [r_idx], dma=True, **kw)

        def bcl(ap3):
            return ap3.broadcast_to([128, NB, 128])

        def bcm(ap2):
            return ap2.unsqueeze(1).broadcast_to([128, NB, 128])

        def seg_load(part, SEG, dirn, r, sp):
            fag = fagd[part]
            for j in range(3):
                if r == 0:
                    S.op("dve", lambda e, j=j: e.memset(raw[j][:, 0:2], 0.0), pwrites=[r_raw[j]])
                else:
                    gather(raw[j][:, 0:2], fag, 2, ("q", part, r, j, "l"), 128, [C.r_xchg], r_raw[j])
                if r == NCORE - 1:
                    S.op("dve", lambda e, j=j, SEG=SEG: e.memset(raw[j][:, SEG + 2:SEG + 4], 0.0), pwrites=[r_raw[j]])
                else:
                    gather(raw[j][:, SEG + 2:SEG + 4], fag, 2, ("q", part, r, j, "r"), 128, [C.r_xchg], r_raw[j])
                gather(raw[j][:, 2:SEG + 2], fag, SEG, ("q", part, r, j, "m"), 128, [C.r_xchg], r_raw[j])
                yield
            gather(a2[:, 0:SEG], fag, SEG, ("a", part, r), 2, [C.r_xchg], r_a2, partial=False)
            gather(b2[:, 0:SEG], fag, SEG, ("b", part, r), 2, [C.r_xchg], r_b2, partial=False)
            if dirn == 1:
                gather(szs[sp][:, 0:SEG], fbg, SEG, ("sz", part, r), 128, [C.r_xchg], r_szs[sp], partial=False)
            yield
            for j in range(3):
                S.op("dve", lambda e, j=j, SEG=SEG: e.tensor_scalar(
                    out=cv[j][:, 0:SEG], in0=raw[j][:, 0:SEG], scalar1=cw[:, j, 0:1], scalar2=None, op0=ALU.mult),
                    reads=[r_raw[j], r_cw], writes=[r_cv[j]])
                yield
                for tap in range(1, 5):
                    S.op("dve", lambda e, j=j, SEG=SEG, tap=tap: e.scalar_tensor_tensor(
                        out=cv[j][:, 0:SEG], in0=raw[j][:, tap:tap + SEG], scalar=cw[:, j, tap:tap + 1],
                        in1=cv[j][:, 0:SEG], op0=ALU.mult, op1=ALU.add),
                        reads=[r_raw[j], r_cw, r_cv[j]], writes=[r_cv[j]])
                    yield
                S.op("act", lambda e, j=j, SEG=SEG: e.activation(out=cv[j][:, 0:SEG], in_=cv[j][:, 0:SEG], func=AF.Silu),
                     reads=[r_cv[j]], writes=[r_cv[j]])
                yield
            for j, dst, scl in ((0, qTb, float(HD) ** -0.5), (1, kTb, 1.0)):
                for c0 in range(0, SEG, 512):
                    S.op("dve", lambda e, j=j, c0=c0: e.tensor_tensor(out=sqs[:], in0=cv[j][:, c0:c0 + 512],
                                                                       in1=cv[j][:, c0:c0 + 512], op=ALU.mult),
                         reads=[r_cv[j]], writes=[r_sqs])
                    S.op("pe", lambda e: e.matmul(pN[:], lhsT=ones_f, rhs=sqs[:], start=True, stop=True),
                         reads=[r_sqs, r_masks], writes=[r_pN])
                    S.op("dve", lambda e: e.tensor_scalar(out=rn[:], in0=pN[:], scalar1=EPS, scalar2=None, op0=ALU.add),
                         reads=[r_pN], writes=[r_rn])
                    S.op("act", lambda e: e.activation(out=rn[:], in_=rn[:], func=AF.Sqrt), reads=[r_rn], writes=[r_rn])
                    yield
                    S.op("dve", lambda e: e.reciprocal(out=rn[:], in_=rn[:]), reads=[r_rn], writes=[r_rn])
                    S.op("dve", lambda e, j=j, c0=c0, scl=scl: e.scalar_tensor_tensor(
                        out=cv[j][:, c0:c0 + 512], in0=cv[j][:, c0:c0 + 512], scalar=scl, in1=rn[:],
                        op0=ALU.mult, op1=ALU.mult), reads=[r_rn, r_cv[j]], writes=[r_cv[j]])
                    yield
                S.op("act", lambda e, j=j, dst=dst, SEG=SEG: e.activation(out=dst[:, 0:SEG], in_=cv[j][:, 0:SEG], func=AF.Copy),
                     reads=[r_cv[j]], writes=[r_cv[j]])
                yield
            S.op("dve", lambda e, SEG=SEG: e.tensor_scalar(out=a2[:, 0:SEG], in0=a2[:, 0:SEG], scalar1=hv[:, 1:2], scalar2=None,
                                                           op0=ALU.add), reads=[r_a2, r_hv], writes=[r_a2])
            S.op("act", lambda e, SEG=SEG: e.activation(out=t2a[:, 0:SEG], in_=a2[:, 0:SEG], func=AF.Abs),
                 reads=[r_a2], writes=[r_t2a])
            yield
            S.op("act", lambda e, SEG=SEG: e.activation(out=t2a[:, 0:SEG], in_=t2a[:, 0:SEG], func=AF.Exp, scale=-1.0),
                 reads=[r_t2a], writes=[r_t2a])
            S.op("act", lambda e, SEG=SEG: e.activation(out=t2a[:, 0:SEG], in_=t2a[:, 0:SEG], func=AF.Ln, bias=1.0),
                 reads=[r_t2a], writes=[r_t2a])
            yield
            S.op("dve", lambda e, SEG=SEG: e.scalar_tensor_tensor(out=t2a[:, 0:SEG], in0=a2[:, 0:SEG], scalar=0.0, in1=t2a[:, 0:SEG],
                                                                  op0=ALU.max, op1=ALU.add), reads=[r_a2, r_t2a], writes=[r_t2a])
            S.op("dve", lambda e, SEG=SEG: e.tensor_scalar(out=t2a[:, 0:SEG], in0=t2a[:, 0:SEG], scalar1=hv[:, 2:3], scalar2=None,
                                                           op0=ALU.mult), reads=[r_t2a, r_hv], writes=[r_t2a])
            S.op("act", lambda e, SEG=SEG: e.activation(out=t2b[:, 0:SEG], in_=b2[:, 0:SEG], func=AF.Sigmoid),
                 reads=[r_b2], writes=[r_t2b])
            yield

        def prep(dirn, c0, bp):
            mi = masks[:, dirn, :]
            ms = masks[:, 2 + dirn, :]
            gsel = gbt[:, :, dirn:dirn + 1]
            bsel = gbt[:, :, 2 + dirn:3 + dirn]

            def trg(e):
                ins = None
                for b in range(NB):
                    cc = c0 + b * 128
                    e.transpose(out=Bk[0][:, b * 4:b * 4 + 2], in_=t2a[0:2, cc:cc + 128], identity=identf[0:2, 0:2])
                    ins = e.transpose(out=Bk[0][:, b * 4 + 2:b * 4 + 4], in_=t2b[0:2, cc:cc + 128], identity=identf[0:2, 0:2])
                return ins
            S.op("pe", trg, reads=[r_t2a, r_t2b], writes=[r_B[0]])
            S.op("dve", lambda e: e.tensor_copy(out=gbt[:].rearrange("p a b -> p (a b)"), in_=Bk[0][:, 0:NB * 4]),
                 reads=[r_B[0]], writes=[r_gbt])
            yield
            S.op("dve", lambda e: e.tensor_tensor(out=g_bc[:], in0=bcm(ones_f), in1=bcl(gsel), op=ALU.mult),
                 reads=[r_gbt, r_masks], writes=[r_g_bc])
            S.op("dve", lambda e: e.tensor_tensor(out=dgb[:], in0=bcm(identf[:]), in1=bcl(bsel), op=ALU.mult),
                 reads=[r_gbt], writes=[r_dgb])
            yield

            def mA(e):
                ins = None
                for b in range(NB):
                    gc_ = gbt[:, b, dirn:dirn + 1]
                    e.matmul(Bk[1][:, b * 128:(b + 1) * 128], lhsT=g_bc[:, b, :], rhs=mi, start=True, stop=True)
                    e.matmul(Bk[0][:, 32 + b * 4:33 + b * 4], lhsT=mi, rhs=gc_, start=True, stop=True)
                    e.matmul(Bk[0][:, 33 + b * 4:34 + b * 4], lhsT=masks[:, 5, :], rhs=gc_, start=True, stop=True)
                    ins = e.matmul(Bk[0][:, 34 + b * 4:36 + b * 4], lhsT=g_bc[:, b, :], rhs=masks[:, 6, 0:2], start=True, stop=True)
                return ins
            S.op("pe", mA, reads=[r_g_bc, r_masks, r_gbt], writes=[r_B[0], r_B[1]])
            yield
            c4 = Bk[0][:, 32:32 + NB * 4].rearrange("p (a b) -> p a b", b=4)
            S.op("dve", lambda e: e.tensor_copy(out=cols[:, :, 0:2], in_=c4[:, :, 0:2]), reads=[r_B[0]], writes=[r_cols])
            S.op("act", lambda e: e.activation(out=cdx[bp][:], in_=c4[:, :, 2:4], func=AF.Exp), reads=[r_B[0]], writes=[r_cdx[bp]])
            yield
            S.op("act", lambda e: e.activation(out=Egc[:].rearrange("p a b -> p (a b)"), in_=Bk[1][:], func=AF.Exp),
                 reads=[r_B[1]], writes=[r_Egc])
            S.op("dve", lambda e: e.tensor_tensor(out=Dm[:], in0=Bk[1][:].rearrange("p (a b) -> p a b", a=NB), in1=bcl(cols[:, :, 0:1]),
                                                  op=ALU.subtract), reads=[r_B[1], r_cols], writes=[r_Dm])
            yield
            S.op("dve", lambda e: e.tensor_scalar(out=Dm[:], in0=Dm[:], scalar1=0.0, scalar2=None, op0=ALU.min), reads=[r_Dm], writes=[r_Dm])
            S.op("act", lambda e: e.activation(out=Dm[:], in_=Dm[:], func=AF.Exp), reads=[r_Dm], writes=[r_Dm])
            yield
            S.op("dve", lambda e: e.tensor_tensor(out=decI[:], in0=Dm[:], in1=bcm(mi), op=ALU.mult), reads=[r_Dm, r_masks], writes=[r_decI])
            S.op("dve", lambda e: e.tensor_tensor(out=decS[:], in0=Dm[:], in1=bcm(ms), op=ALU.mult), reads=[r_Dm, r_masks], writes=[r_decS])
            yield
            S.op("act", lambda e: e.activation(out=cols[:, :, 2:3], in_=cols[:, :, 0:1], func=AF.Exp), reads=[r_cols], pwrites=[r_cols])
            S.op("dve", lambda e: e.tensor_tensor(out=cols[:, :, 3:4], in0=cols[:, :, 1:2], in1=cols[:, :, 0:1], op=ALU.subtract),
                 reads=[r_cols], pwrites=[r_cols])
            yield
            S.op("act", lambda e: e.activation(out=cols[:, :, 3:4], in_=cols[:, :, 3:4], func=AF.Exp), reads=[r_cols], pwrites=[r_cols])
            S.op("dve", lambda e: e.tensor_tensor(out=cols[:, :, 4:5], in0=cols[:, :, 2:3], in1=bsel, op=ALU.mult),
                 reads=[r_cols, r_gbt], pwrites=[r_cols])
            yield

            def mB(e):
                ins = None
                for b in range(NB):
                    cc = c0 + b * 128
                    e.matmul(Bk[2][:, b * 128:(b + 1) * 128], lhsT=kTb[:, cc:cc + 128], rhs=kTb[:, cc:cc + 128], start=True, stop=True)
                    e.matmul(Bk[3][:, b * 128:(b + 1) * 128], lhsT=kTb[:, cc:cc + 128], rhs=qTb[:, cc:cc + 128], start=True, stop=True)
                    ins = e.matmul(Bk[4][:, b * 128:(b + 1) * 128], lhsT=ones_f, rhs=dgb[:, b, :], start=True, stop=True)
                return ins
            S.op("pe", mB, reads=[r_cv[0], r_cv[1], r_dgb, r_masks], writes=[r_B[2], r_B[3], r_B[4]])
            yield
            v3 = lambda bank: bank[:].rearrange("p (a b) -> p a b", a=NB)
            S.op("dve", lambda e: e.tensor_tensor(out=AQT[bp][:], in0=v3(Bk[3]), in1=decI[:], op=ALU.mult),
                 reads=[r_B[3], r_decI], writes=[r_AQT[bp]])
            S.op("dve", lambda e: e.tensor_tensor(out=decS[:], in0=v3(Bk[2]), in1=decS[:], op=ALU.mult),
                 reads=[r_B[2], r_decS], writes=[r_decS])
            yield
            S.op("dve", lambda e: e.scalar_tensor_tensor(out=XA_[0][:], in0=decS[:], scalar=-1.0, in1=v3(Bk[4]),
                                                         op0=ALU.mult, op1=ALU.mult), reads=[r_B[4], r_decS], writes=[r_XA[0]])
            yield

            def trX(e):
                ins = None
                for b in range(NB):
                    ins = e.transpose(out=Bk[0][:, b * 128:(b + 1) * 128], in_=XA_[0][:, b, :], identity=identf[:])
                return ins
            S.op("pe", trX, reads=[r_XA[0]], writes=[r_B[0]])
            S.op("act", lambda e: e.activation(out=XT_[0][:].rearrange("p a b -> p (a b)"), in_=Bk[0][:], func=AF.Copy),
                 reads=[r_B[0]], writes=[r_XT[0]])
            S.op("dve", lambda e: e.tensor_tensor(out=Pm[:], in0=XA_[0][:], in1=bcm(identf[:]), op=ALU.add),
                 reads=[r_XA[0]], writes=[r_Pm])
            yield
            for lv in range(1, 6):
                s_, d_ = (lv - 1) % 2, lv % 2

                def sqm(e, s_=s_, lv=lv):
                    ins = None
                    for b in range(NB):
                        ins = e.matmul(Bk[2][:, b * 128:(b + 1) * 128], lhsT=XA_[s_][:, b, :], rhs=XT_[s_][:, b, :], start=True, stop=True)
                        if lv < 5:
                            ins = e.matmul(Bk[1][:, b * 128:(b + 1) * 128], lhsT=XT_[s_][:, b, :], rhs=XA_[s_][:, b, :], start=True, stop=True)
                    return ins
                S.op("pe", sqm, reads=[r_XA[s_], r_XT[s_]], writes=[r_B[1], r_B[2]])
                S.op("act", lambda e, d_=d_: e.activation(out=XT_[d_][:].rearrange("p a b -> p (a b)"), in_=Bk[2][:], func=AF.Copy),
                     reads=[r_B[2]], writes=[r_XT[d_]])
                if lv < 5:
                    S.op("dve", lambda e, d_=d_: e.tensor_copy(out=XA_[d_][:].rearrange("p a b -> p (a b)"), in_=Bk[1][:]),
                         reads=[r_B[1]], writes=[r_XA[d_]])
                yield

                def pmm(e, d_=d_):
                    ins = None
                    for b in range(NB):
                        ins = e.matmul(Bk[0][:, b * 128:(b + 1) * 128], lhsT=XT_[d_][:, b, :], rhs=Pm[:, b, :], start=True, stop=True)
                    return ins
                S.op("pe", pmm, reads=[r_XT[d_], r_Pm], writes=[r_B[0]])
                S.op("dve", lambda e: e.tensor_tensor(out=Pm[:].rearrange("p a b -> p (a b)"), in0=Pm[:].rearrange("p a b -> p (a b)"),
                                                      in1=Bk[0][:], op=ALU.add), reads=[r_B[0], r_Pm], writes=[r_Pm])
                yield
            S.op("act", lambda e: e.activation(out=Pb[:], in_=Pm[:], func=AF.Copy), reads=[r_Pm], writes=[r_Pb])

            def trkv(e):
                ins = None
                for b in range(NB):
                    cc = c0 + b * 128
                    e.transpose(out=Bk[3][:, b * 128:(b + 1) * 128], in_=cv[1][:, cc:cc + 128], identity=identf[:])
                    ins = e.transpose(out=Bk[4][:, b * 128:(b + 1) * 128], in_=cv[2][:, cc:cc + 128], identity=identf[:])
                return ins
            S.op("pe", trkv, reads=[r_cv[1], r_cv[2]], writes=[r_B[3], r_B[4]])
            yield
            S.op("dve", lambda e: e.tensor_tensor(out=bv[:], in0=v3(Bk[4]), in1=bcl(bsel), op=ALU.mult),
                 reads=[r_B[4], r_gbt], writes=[r_bv])
            S.op("dve", lambda e: e.tensor_tensor(out=bke[:], in0=v3(Bk[3]), in1=bcl(cols[:, :, 4:5]), op=ALU.mult),
                 reads=[r_B[3], r_cols], writes=[r_bke])
            yield
            S.op("dve", lambda e: e.tensor_tensor(out=kdc[bp][:], in0=v3(Bk[3]), in1=bcl(cols[:, :, 3:4]), op=ALU.mult),
                 reads=[r_B[3], r_cols], writes=[r_kdc[bp]])
            S.op("dve", lambda e: e.tensor_tensor(out=qdT[bp][:].rearrange("p a b -> p (a b)"), in0=cv[0][:, c0:c0 + W],
                                                  in1=Egc[:].rearrange("p a b -> p (a b)"), op=ALU.mult),
                 reads=[r_cv[0], r_Egc], writes=[r_qdT[bp]])
            yield

            def muw(e):
                ins = None
                for b in range(NB):
                    e.matmul(Bk[1][:, b * 128:(b + 1) * 128], lhsT=Pb[:, b, :], rhs=bv[:, b, :], start=True, stop=True)
                    ins = e.matmul(Bk[2][:, b * 128:(b + 1) * 128], lhsT=bke[:, b, :], rhs=Pb[:, b, :], start=True, stop=True)
                return ins
            S.op("pe", muw, reads=[r_Pb, r_bv, r_bke], writes=[r_B[1], r_B[2]])
            S.op("dve", lambda e: e.tensor_copy(out=u_t[bp][:].rearrange("p a b -> p (a b)"), in_=Bk[1][:]), reads=[r_B[1]], writes=[r_u[bp]])
            S.op("act", lambda e: e.activation(out=wT[bp][:].rearrange("p a b -> p (a b)"), in_=Bk[2][:], func=AF.Copy),
                 reads=[r_B[2]], writes=[r_wT[bp]])
            yield

        def scan(dirn, c0, bp, sp, gt_base, last_of_seg, seg_dma):
            order = range(NB) if dirn == 0 else range(NB - 1, -1, -1)
            for b in order:
                ob = b % 2
                for ch in ((0, 1) if dirn == 0 else (1, 0)):
                    cs = slice(ch * 64, ch * 64 + 64)
                    S.op("pe", lambda e, cs=cs, b=b: e.matmul(pS[cs, 0:128], lhsT=wT[bp][:, b, cs], rhs=Sb[:], start=True, stop=True),
                         reads=[r_wT[bp], r_Sb], writes=[r_pS])
                    S.op("dve", lambda e, cs=cs, b=b: e.tensor_tensor(out=vn[cs, :], in0=u_t[bp][cs, b, :], in1=pS[cs, 0:128], op=ALU.subtract),
                         reads=[r_pS, r_u[bp]], writes=[r_vn])
                    yield

                    S.op("pe", lambda e, cs=cs, b=b: e.matmul(pD[:, 0:128], lhsT=kdc[bp][cs, b, :], rhs=vn[cs, :], start=True, stop=True),
                         reads=[r_kdc[bp], r_vn], writes=[r_pD])

                    def mo(e, cs=cs, b=b):
                        e.matmul(pS[cs, 128:256], lhsT=qdT[bp][:, b, cs], rhs=Sb[:], start=True, stop=False)
                        return e.matmul(pS[cs, 128:256], lhsT=AQT[bp][cs, b, cs], rhs=vn[cs, :], start=False, stop=True)
                    S.op("pe", mo, reads=[r_qdT[bp], r_Sb, r_AQT[bp], r_vn], writes=[r_pS])
                    yield
                    S.op("dve", lambda e, ch=ch, b=b: e.scalar_tensor_tensor(out=Sst[:], in0=Sst[:], scalar=cdx[bp][:, b, ch:ch + 1], in1=pD[:, 0:128],
                                                                           op0=ALU.mult, op1=ALU.add), reads=[r_pD, r_cdx[bp], r_S], writes=[r_S])
                    S.op("dve", lambda e: e.tensor_copy(out=Sb[:], in_=Sst[:]), reads=[r_S], writes=[r_Sb])
                    S.op("act", lambda e, cs=cs, ob=ob: e.activation(out=o_t[ob][cs, :], in_=pS[cs, 128:256], func=AF.Copy),
                         reads=[r_pS], pwrites=[r_o[ob]])
                    yield
                gt0 = gt_base + c0 + b * 128
                cc = c0 + b * 128
                if dirn == 0:
                    S.op("sp", lambda e, gt0=gt0, ob=ob: e.dma_start(out=of_d[gt0:gt0 + 128, :], in_=o_t[ob][:]), reads=[r_o[ob]], pwrites=[C.r_of], dma=True)
                else:
                    S.op("sp", lambda e, gt0=gt0: e.dma_start(out=of_t[:], in_=of_d[gt0:gt0 + 128, :]), reads=[C.r_of], writes=[r_of], dma=True)
                    S.op("dve", lambda e, ob=ob: e.tensor_tensor(out=of_t[:], in0=of_t[:], in1=o_t[ob][:], op=ALU.add), reads=[r_o[ob], r_of], writes=[r_of])
                    yield
                    S.op("dve", lambda e: e.tensor_tensor(out=osq[:], in0=of_t[:], in1=of_t[:], op=ALU.mult), reads=[r_of], writes=[r_osq])
                    S.op("dve", lambda e: e.tensor_reduce(out=oss[:], in_=osq[:], axis=AX.X, op=ALU.add), reads=[r_osq], writes=[r_oss])
                    yield
                    S.op("dve", lambda e: e.tensor_scalar(out=oss[:], in0=oss[:], scalar1=1.0 / 128, scalar2=EPS, op0=ALU.mult, op1=ALU.add),
                         reads=[r_oss], writes=[r_oss])
                    S.op("act", lambda e: e.activation(out=oss[:], in_=oss[:], func=AF.Sqrt), reads=[r_oss], writes=[r_oss])
                    yield
                    S.op("dve", lambda e: e.reciprocal(out=oss[:], in_=oss[:]), reads=[r_oss], writes=[r_oss])
                    S.op("dve", lambda e: e.scalar_tensor_tensor(out=on_t[:], in0=of_t[:], scalar=oss[:, 0:1], in1=gbc[:],
                                                                 op0=ALU.mult, op1=ALU.mult), reads=[r_of, r_oss, r_gbc], writes=[r_on])
                    yield
                    S.op("pe", lambda e: e.transpose(out=pN[:, 0:128], in_=on_t[:], identity=identf[:]), reads=[r_on], writes=[r_pN])
                    S.op("dve", lambda e, cc=cc: e.tensor_tensor(out=ogs[sp][:, cc:cc + 128], in0=pN[:, 0:128], in1=szs[sp][:, cc:cc + 128], op=ALU.mult),
                         reads=[r_pN, r_szs[sp]], pwrites=[r_ogs[sp]])
                yield
            if last_of_seg and dirn == 1:
                seg_dma()
                yield

        def interleave(gens):
            gens = [g for g in gens if g is not None]
            while gens:
                for g in list(gens):
                    try:
                        next(g)
                    except StopIteration:
                        gens.remove(g)

        def chain(*gs):
            for g in gs:
                yield from g

        segn = 0
        bn = 0
        for part, SEG, L, gbase in (("P", TP, LP, 0), ("S", TS, LS, LP)):
            ogo, _ = OGR[part]
            og_v = fog[ogo:ogo + 128 * L].rearrange("(f t) -> f t", f=128)
            for dirn in (0, 1):
                S.op("dve", lambda e: e.memset(Sst[:], 0.0), writes=[r_S])
                S.op("dve", lambda e: e.memset(Sb[:], 0.0), writes=[r_Sb])
                pending = None
                seg_order = range(NCORE) if dirn == 0 else range(NCORE - 1, -1, -1)
                for r in seg_order:
                    sp = segn % 2; segn += 1
                    nbat = SEG // W
                    bat_order = range(nbat) if dirn == 0 else range(nbat - 1, -1, -1)
                    first = True
                    for bi_, bi in enumerate(bat_order):
                        bp = bn % 2; bn += 1
                        c0 = bi * W
                        pg = prep(dirn, c0, bp)
                        if first:
                            pg = chain(seg_load(part, SEG, dirn, r, sp), pg)
                            first = False
                        if GDN_INTERLEAVE:
                            interleave([pending, pg])
                        else:
                            interleave([pending])
                            interleave([pg])

                        def seg_dma(r=r, SEG=SEG, sp=sp, og_v=og_v):
                            gs = r * SEG
                            S.op("sp", lambda e: e.dma_start(out=og_v[:, gs:gs + SEG], in_=ogs[sp][:, 0:SEG]),
                                 reads=[r_ogs[sp]], pwrites=[C.r_ogx], dma=True)
                        pending = scan(dirn, c0, bp, sp, gbase + r * SEG, bi_ == nbat - 1, seg_dma)
                interleave([pending])
        S.barrier()
        S.flush()


def emit_wout(C, nc, OG_g, mixT_d, h1_d, h2_d, w_out, idx_t, r_idx, LP, LS):
    S = C.S
    TP, TS = LP // NCORE, LS // NCORE
    T = TP + TS
    KI = idx_layout(LP, LS)
    fog = flat(OG_g).rearrange("(m f) -> m f", f=512)
    with ExitStack() as st:
        mixT = _tile(st, nc, "w_mixT", [128, 16, 512], BF16); r_mixT = Res()
        ws = [_tile(st, nc, "w_ws%d" % i, [128, KD, 256], BF16) for i in range(2)]
        r_ws = [Res() for _ in range(2)]
        ht = [_tile(st, nc, "w_ht%d" % i, [128, D], F32) for i in range(4)]
        r_ht = [Res() for _ in range(4)]
        pw = [_psum(st, nc, "w_pw%d" % i, [128, 512], F32) for i in range(4)]
        r_pw = [Res(excl=True) for _ in range(4)]
        wrr = 0
        pc = 0
        for g in range(T // 512):
            t0 = g * 512
            for h in range(8):
                col = KI[("og", g, h)]
                S.op("pool", lambda e, h=h, col=col: e.indirect_dma_start(
                    out=mixT[:, h, :], out_offset=None, in_=fog,
                    in_offset=bass.IndirectOffsetOnAxis(ap=idx_t[0:128, col:col + 1], axis=0)),
                    reads=[C.r_ogg, r_idx], pwrites=[r_mixT], dma=True)
            S.op("sp", lambda e, t0=t0: e.dma_start(out=mixT[:, 8:16, :], in_=mixT_d[:, :, t0:t0 + 512].rearrange("h f t -> f h t")),
                 reads=[C.r_mixA], pwrites=[r_mixT], dma=True)
            for i in range(4):
                S.op("sp", lambda e, i=i, t0=t0: e.dma_start(out=ht[i][:], in_=h1_d[t0 + i * 128:t0 + (i + 1) * 128, :]),
                     reads=[C.r_h1], writes=[r_ht[i]], dma=True)
            for dc in range(8):
                wb = wrr % 2; wrr += 1
                S.op("pool", lambda e, wb=wb, dc=dc: e.dma_start(
                    out=ws[wb][:], in_=w_out[:, dc * 256:(dc + 1) * 256].rearrange("(k p) c -> p k c", p=128)),
                    writes=[r_ws[wb]], dma=True)
                for i in range(4):
                    pi = pc % 4; pc += 1

                    def mm(e, wb=wb, i=i, pi=pi):
                        ins = None
                        for k in range(KD):
                            ins = e.matmul(pw[pi][:, 0:256], lhsT=mixT[:, k, i * 128:(i + 1) * 128], rhs=ws[wb][:, k, :],
                                           start=(k == 0), stop=(k == KD - 1))
                        return ins
                    S.op("pe", mm, reads=[r_ws[wb], r_mixT], writes=[r_pw[pi]])
                    S.op("dve", lambda e, i=i, dc=dc, pi=pi: e.tensor_tensor(
                        out=ht[i][:, dc * 256:(dc + 1) * 256], in0=ht[i][:, dc * 256:(dc + 1) * 256], in1=pw[pi][:, 0:256], op=ALU.add),
                        reads=[r_pw[pi]], writes=[r_ht[i]])
            for i in range(4):
                S.op("sp", lambda e, i=i, t0=t0: e.dma_start(out=h2_d[t0 + i * 128:t0 + (i + 1) * 128, :], in_=ht[i][:]),
                     reads=[r_ht[i]], pwrites=[C.r_h2], dma=True)
        S.barrier()
        S.flush()


def emit_final(C, nc, h3_d, y_d, fnorm_d, T):
    S = C.S
    with ExitStack() as st:
        gb = _tile(st, nc, "z_gb", [128, D], F32); r_gb = Res()
        xt = [_tile(st, nc, "z_xt%d" % i, [128, D], F32) for i in range(2)]
        r_xt = [Res() for _ in range(2)]
        yt = [_tile(st, nc, "z_yt%d" % i, [128, D], F32) for i in range(2)]
        r_yt = [Res() for _ in range(2)]
        junk = _tile(st, nc, "z_junk", [128, D], BF16); r_junk = Res()
        ss = _tile(st, nc, "z_ss", [128, 2], F32); r_ss = [Res() for _ in range(2)]
        S.op("sp", lambda e: e.dma_start(out=gb[:], in_=fnorm_d.partition_broadcast(128)), writes=[r_gb], dma=True)
        for i in range(T // 128):
            b = i % 2
            S.op("sp", lambda e, b=b, i=i: e.dma_start(out=xt[b][:], in_=h3_d[i * 128:(i + 1) * 128, :]),
                 reads=[C.r_h3], writes=[r_xt[b]], dma=True)
            S.op("act", lambda e, b=b: e.activation(out=junk[:], in_=xt[b][:], func=AF.Square, accum_out=ss[:, b:b + 1]),
                 reads=[r_xt[b]], writes=[r_junk, r_ss[b]])
            S.op("dve", lambda e, b=b: e.tensor_scalar(out=ss[:, b:b + 1], in0=ss[:, b:b + 1], scalar1=1.0 / D, scalar2=EPS,
                                                       op0=ALU.mult, op1=ALU.add), reads=[r_ss[b]], writes=[r_ss[b]])
            S.op("act", lambda e, b=b: e.activation(out=ss[:, b:b + 1], in_=ss[:, b:b + 1], func=AF.Sqrt), reads=[r_ss[b]], writes=[r_ss[b]])
            S.op("dve", lambda e, b=b: e.reciprocal(out=ss[:, b:b + 1], in_=ss[:, b:b + 1]), reads=[r_ss[b]], writes=[r_ss[b]])
            S.op("dve", lambda e, b=b: e.scalar_tensor_tensor(out=yt[b][:], in0=xt[b][:], scalar=ss[:, b:b + 1], in1=gb[:],
                                                              op0=ALU.mult, op1=ALU.mult), reads=[r_xt[b], r_ss[b], r_gb], writes=[r_yt[b]])
            S.op("sp", lambda e, b=b, i=i: e.dma_start(out=y_d[i * 128:(i + 1) * 128, :], in_=yt[b][:]),
                 reads=[r_yt[b]], pwrites=[C.r_y], dma=True)
        S.barrier()
        S.flush()


def _pad2048(n):
    return (n + 2047) // 2048 * 2048


def build_program(LP, LS, stop_after=None):
    TP, TS = LP // NCORE, LS // NCORE
    T = TP + TS
    LT = LP + LS
    nc = bass.Bass("TRN2", target_bir_lowering=False)

    def din(name, shape, dt=F32):
        return nc.dram_tensor(name, list(shape), dt, kind="ExternalInput").ap()

    def dint(name, shape, dt=F32):
        return nc.dram_tensor(name, list(shape), dt).ap()

    x_d = din("x", [T, D])
    w1g, w1u, w1d = din("w1g", [D, DFF]), din("w1u", [D, DFF]), din("w1d", [DFF, D])
    w2g, w2u, w2d = din("w2g", [D, DFF]), din("w2u", [D, DFF]), din("w2d", [DFF, D])
    w_in, w_out = din("w_in", [D, IN_DIM]), din("w_out", [D, D])
    gains = din("gains", [4, 128, KD])
    consts = din("consts", [128, 128])
    KI = idx_layout(LP, LS)
    idx_d = din("idx", [128, len(KI)], I32)
    vecs_d = din("vecs", [8, 128])
    hv_d = din("hv", [2, 4])
    convw_d = din("convw", [128, 3, 5])
    masks_d = din("masks", [128, 7, 128])
    rope_d = din("rope", [T, 128])
    fnorm_d = din("fnorm", [D])
    y_d = nc.dram_tensor("y", [T, D], F32, kind="ExternalOutput").ap()
    h1_d, h2_d, h3_d = dint("h1_s", [T, D]), dint("h2_s", [T, D]), dint("h3_s", [T, D])
    RA, NA, RB, NB = region_sizes(TP, TS)
    NAp, NBp = {p: _pad2048(NA[p]) for p in NA}, _pad2048(NB)
    OGR, NO = og_sizes(LP, LS)
    NOp = _pad2048(NO)
    XA_x = {p: dint("XA_x" + p, [NAp[p] // 2048, 2048]) for p in NAp}
    XA_g = {p: dint("XA_g" + p, [NCORE * NAp[p] // 2048, 2048]) for p in NAp}
    XB_x, XB_g = dint("XB_x", [NBp // 2048, 2048], BF16), dint("XB_g", [NCORE * NBp // 2048, 2048], BF16)
    OG_x, OG_g = dint("OG_x", [NOp // 2048, 2048], BF16), dint("OG_g", [NCORE * NOp // 2048, 2048], BF16)
    qT_d = dint("qT_s", [8, 128, T], BF16)
    mixT_d = dint("mixT_s", [8, 128, T], BF16)
    of_d = dint("of_s", [LT, 128])

    C = Ctx()
    C.NAp, C.NBp, C.NOp = NAp, NBp, NOp
    C.pending_cc = []
    with ExitStack() as top:
        S = Sched(nc, top, 8, 8)
        C.S = S
        for i in range(4):
            S.sem["cc%d" % i] = top.enter_context(nc.semaphore("cc%d" % i))
            S.cnt["cc%d" % i] = 0
        C.ident_f = _tile(top, nc, "ident_f", [128, 128], F32)
        C.ident_b = _tile(top, nc, "ident_b", [128, 128], BF16)
        idx_t = _tile(top, nc, "idx_t", [128, len(KI)], I32)
        r_const = Res(); r_idx = Res()
        for nm in ("r_xch", "r_xchg", "r_qT", "r_mixA", "r_of", "r_ogx", "r_ogg", "r_h1", "r_h2", "r_h3", "r_y"):
            setattr(C, nm, Res())
        S.block = top.enter_context(nc.Block())
        S.op("sp", lambda e: e.dma_start(out=C.ident_f[:], in_=consts), writes=[r_const], dma=True)
        S.op("sp", lambda e: e.dma_start(out=idx_t[:], in_=idx_d), writes=[r_idx], dma=True)
        S.op("dve", lambda e: e.tensor_copy(out=C.ident_b[:], in_=C.ident_f[:]), reads=[r_const], writes=[r_const])
        S.barrier()

        def allgather(i, src, dst, wait=True):
            key = "cc%d" % i
            deps = {k: v for k, v in S.all_events().items() if not k.startswith("cc")}
            S._wait("pool", deps)
            sem = S.sem[key]
            S.lists["pool"].append(lambda e: e.collective_compute(
                "AllGather", ALU.bypass, replica_groups=[list(range(NCORE))], ins=[src.opt()], outs=[dst.opt()]).then_inc(sem, 1))
            if wait:
                S.cnt[key] += 1
                S.barrier()
            else:
                C.pending_cc.append(key)

        if stop_after == "ffn1":
            emit_ffn(C, nc, x_d, y_d, gains[0], w1g, w1u, w1d, T)
        else:
            emit_ffn(C, nc, x_d, h1_d, gains[0], w1g, w1u, w1d, T)
            emit_proj(C, nc, h1_d, gains[1], w_in, XA_x, XB_x, qT_d, rope_d, vecs_d, T, TP, TS)
            allgather(0, XB_x, XB_g)
            allgather(1, XA_x["P"], XA_g["P"], wait=False)
            allgather(3, XA_x["S"], XA_g["S"], wait=False)
            emit_attn(C, nc, XB_g, qT_d, mixT_d, vecs_d, T, TP, TS)
            for key in C.pending_cc:
                S.cnt[key] += 1
            C.pending_cc = []
            S.barrier()
            emit_gdn(C, nc, XA_g, XB_g, OG_x, of_d, idx_t, r_idx, hv_d, vecs_d, convw_d, masks_d, LP, LS)
            allgather(2, OG_x, OG_g)
            if stop_after == "h2":
                emit_wout(C, nc, OG_g, mixT_d, h1_d, y_d, w_out, idx_t, r_idx, LP, LS)
            else:
                emit_wout(C, nc, OG_g, mixT_d, h1_d, h2_d, w_out, idx_t, r_idx, LP, LS)
                emit_ffn(C, nc, h2_d, h3_d, gains[2], w2g, w2u, w2d, T)
                emit_final(C, nc, h3_d, y_d, fnorm_d, T)
        S.barrier()
        S.flush()
    return nc


_PROG_CACHE = {}


def _gain_cols(g):
    return np.ascontiguousarray(np.asarray(g, np.float32).reshape(KD, 128).T)


def _masks():
    t = np.arange(128)
    same = (t[:, None] // 64) == (t[None, :] // 64)
    m = np.zeros((128, 7, 128), np.float32)
    m[:, 0] = same & (t[:, None] <= t[None, :])
    m[:, 1] = same & (t[:, None] >= t[None, :])
    m[:, 2] = same & (t[:, None] < t[None, :])
    m[:, 3] = same & (t[:, None] > t[None, :])
    m[:, 4] = 1.0
    m[:, 5] = same
    m[:, 6, 0] = t < 64
    m[:, 6, 1] = t >= 64
    return m


def _rope_table(pos):
    pos = np.asarray(pos)
    row = (pos // 64).astype(np.float32)
    col = (pos % 64).astype(np.float32)
    freqs = (np.float32(10000.0) ** (-np.arange(0, 64, 2, dtype=np.float32) / np.float32(64))).astype(np.float32)
    ar = (row[:, None] * freqs[None, :]).astype(np.float32)
    ac = (col[:, None] * freqs[None, :]).astype(np.float32)
    return np.concatenate([np.cos(ar), np.cos(ac), np.sin(ar), np.sin(ac)], 1).astype(np.float32)


def _idx_table(c, LP, LS):
    TP, TS = LP // NCORE, LS // NCORE
    RA, NA, RB, NB = region_sizes(TP, TS)
    NApd, NBp = {p: _pad2048(NA[p]) for p in NA}, _pad2048(NB)
    OGR, NO = og_sizes(LP, LS)
    NOp = _pad2048(NO)
    KI = idx_layout(LP, LS)
    tab = np.zeros((128, len(KI)), np.int64)
    p = np.arange(128)
    for key, col in KI.items():
        if key[0] == "q":
            _, part, r, j, kind = key
            SEG = TP if part == "P" else TS
            off = RA["qkv" + part][0]
            NAp = NApd[part]
            base = off + ((c * 3 + j) * 128 + p) * SEG
            if kind == "m":
                tab[:, col] = (r * NAp + base) // SEG
            elif kind == "l":
                tab[:, col] = ((max(r - 1, 0)) * NAp + base + SEG - 2) // 2
            else:
                tab[:, col] = ((min(r + 1, NCORE - 1)) * NAp + base) // 2
        elif key[0] == "sz":
            _, part, r = key
            SEG = TP if part == "P" else TS
            tab[:, col] = (r * NBp + RB["sz" + part][0] + (c * 128 + p) * SEG) // SEG
        elif key[0] in ("a", "b"):
            _, part, r = key
            SEG = TP if part == "P" else TS
            rowb = 0 if key[0] == "a" else 16
            pp = np.minimum(p, 1)
            tab[:, col] = (r * NApd[part] + RA["ab" + part][0] + (rowb + pp * 8 + c) * SEG) // SEG
        else:
            _, g, h = key
            t0 = g * 512
            if t0 < TP:
                L, Tp, o, off = LP, TP, t0, OGR["P"][0]
            else:
                L, Tp, o, off = LS, TS, t0 - TP, OGR["S"][0]
            tab[:, col] = (h * NOp + off + p * L + c * Tp + o) // 512
    assert tab.max() < 2 ** 31
    return tab.astype(np.int32)


def run(inp, LP, LS, stop_after=None):
    TP, TS = LP // NCORE, LS // NCORE
    key = (LP, LS, stop_after)
    if key not in _PROG_CACHE:
        _PROG_CACHE[key] = build_program(LP, LS, stop_after)
    nc = _PROG_CACHE[key]
    f = lambda a: np.ascontiguousarray(np.asarray(a, np.float32))
    xp, xs = f(inp["x_prompt"])[0], f(inp["x_sample"])[0]
    vecs = np.zeros((8, 128), np.float32)
    vecs[0], vecs[1], vecs[2], vecs[3] = f(inp["q_norm"])[0], f(inp["k_norm"])[0], f(inp["attn_out_norm"])[0], f(inp["gdn_out_norm"])[0]
    cwf = f(inp["conv_w"])[0]
    shared = {
        "w1g": f(inp["ffn1_w_gate"])[0], "w1u": f(inp["ffn1_w_up"])[0], "w1d": f(inp["ffn1_w_down"])[0],
        "w2g": f(inp["ffn2_w_gate"])[0], "w2u": f(inp["ffn2_w_up"])[0], "w2d": f(inp["ffn2_w_down"])[0],
        "w_in": f(inp["w_in"])[0], "w_out": f(inp["w_out"])[0],
        "gains": np.stack([_gain_cols(inp["ffn1_norm"][0]), _gain_cols(inp["mix_norm"][0]),
                           _gain_cols(inp["ffn2_norm"][0]), _gain_cols(inp["final_norm"])]),
        "consts": np.eye(128, dtype=np.float32), "vecs": vecs, "masks": _masks(), "fnorm": f(inp["final_norm"]),
    }
    in_maps = []
    for c in range(NCORE):
        m = dict(shared)
        m["x"] = np.ascontiguousarray(np.concatenate([xp[c * TP:(c + 1) * TP], xs[c * TS:(c + 1) * TS]], 0))
        if stop_after != "ffn1":
            m["idx"] = _idx_table(c, LP, LS)
            hv = np.zeros((2, 4), np.float32)
            hv[0, 0], hv[0, 1] = f(inp["a_log_fwd"])[0, c], f(inp["dt_bias_fwd"])[0, c]
            hv[1, 0], hv[1, 1] = f(inp["a_log_bwd"])[0, c], f(inp["dt_bias_bwd"])[0, c]
            m["hv"] = hv
            m["convw"] = np.ascontiguousarray(cwf.reshape(5, 3, 8, 128)[:, :, c, :].transpose(2, 1, 0))
            m["rope"] = np.concatenate([_rope_table(np.arange(c * TP, (c + 1) * TP)), _rope_table(np.arange(c * TS, (c + 1) * TS))], 0)
        in_maps.append(m)
    if stop_after == "ffn1":
        keep = ("x", "w1g", "w1u", "w1d", "gains", "consts")
    else:
        keep = None
    if keep is not None:
        in_maps = [{k: v for k, v in m.items() if k in keep} for m in in_maps]
    res = run_bass_kernel_spmd(nc, in_maps, core_ids=list(range(NCORE)))
    ys = [np.asarray(r["y"]) for r in res.results]
    yp = np.concatenate([y[:TP] for y in ys], 0)[None]
    ysm = np.concatenate([y[TP:] for y in ys], 0)[None]
    return (yp.astype(np.float32), ysm.astype(np.float32))


def kernel(**inputs):
    return run(inputs, 16384, 8192)
```

```python
import numpy as np
from contextlib import ExitStack
import concourse.bass as bass
import concourse.mybir as mybir
from concourse.bass_utils import run_bass_kernel_spmd

F32 = mybir.dt.float32
BF16 = mybir.dt.bfloat16
I32 = mybir.dt.int32
AF = mybir.ActivationFunctionType
ALU = mybir.AluOpType
AX = mybir.AxisListType

D = 2048
DFF = 5632
NFF = DFF // 128
IN_DIM = 5664
HD = 128
NCORE = 8
EPS = 1e-6
KD = D // 128
GDN_INTERLEAVE = True


class Res:
    __slots__ = ("w", "r", "name", "excl")

    def __init__(self, name="", excl=False):
        self.w = {}
        self.r = {}
        self.name = name
        self.excl = excl


class Sched:
    ENG = ("sp", "act", "pe", "dve", "pool")

    def __init__(self, nc, stack, n_dma_sp=24, n_dma_pool=24):
        self.nc = nc
        self.lists = {k: [] for k in self.ENG}
        self.sem = {}
        self.cnt = {}
        for k in self.ENG:
            self.sem[k] = stack.enter_context(nc.semaphore("s_" + k))
            self.cnt[k] = 0
        self.dma_slots = {"sp": [], "pool": []}
        for q, n in (("sp", n_dma_sp), ("pool", n_dma_pool)):
            for i in range(n):
                key = "d_%s_%d" % (q, i)
                self.sem[key] = stack.enter_context(nc.semaphore(key))
                self.cnt[key] = 0
                self.dma_slots[q].append(key)
        self.dma_rr = {"sp": 0, "pool": 0}
        self.known = {k: {} for k in self.ENG}
        self.block = None

    def _wait(self, eng, deps):
        kn = self.known[eng]
        for k, v in deps.items():
            if v <= 0:
                continue
            if k == eng and eng == "pe":
                continue
            if kn.get(k, 0) >= v:
                continue
            kn[k] = v
            sem = self.sem[k]
            self.lists[eng].append(lambda e, sem=sem, v=v: e.wait_ge(sem, v))

    @staticmethod
    def _merge(d, k, v):
        if d.get(k, 0) < v:
            d[k] = v

    def op(self, eng, fn, reads=(), writes=(), pwrites=(), dma=False):
        deps = {}
        for r in reads:
            for k, v in r.w.items():
                self._merge(deps, k, v)
            if r.excl:
                for k, v in r.r.items():
                    if k != eng:
                        self._merge(deps, k, v)
        for w in writes:
            for k, v in w.w.items():
                self._merge(deps, k, v)
            for k, v in w.r.items():
                self._merge(deps, k, v)
        for w in pwrites:
            for k, v in w.r.items():
                self._merge(deps, k, v)
        if dma:
            slots = self.dma_slots[eng]
            key = slots[self.dma_rr[eng] % len(slots)]
            self.dma_rr[eng] += 1
            self._merge(deps, key, self.cnt[key])
            self._wait(eng, deps)
            self.cnt[key] += 16
            ev = (key, self.cnt[key])
            sem = self.sem[key]
            self.lists[eng].append(lambda e, fn=fn, sem=sem: fn(e).then_inc(sem, 16))
        else:
            self._wait(eng, deps)
            self.cnt[eng] += 1
            ev = (eng, self.cnt[eng])
            sem = self.sem[eng]
            self.lists[eng].append(lambda e, fn=fn, sem=sem: fn(e).then_inc(sem, 1))
        for r in reads:
            self._merge(r.r, ev[0], ev[1])
        for w in writes:
            w.w = {ev[0]: ev[1]}
            w.r = {}
        for w in pwrites:
            self._merge(w.w, ev[0], ev[1])
        return ev

    def all_events(self):
        return {k: v for k, v in self.cnt.items() if v > 0}

    def barrier(self):
        ev = self.all_events()
        for eng in self.ENG:
            self._wait(eng, dict(ev))

    def flush(self):
        blk = self.block
        for key, dec in (("sp", blk.sync), ("act", blk.scalar), ("pe", blk.tensor),
                         ("dve", blk.vector), ("pool", blk.gpsimd)):
            lst = self.lists[key]
            if not lst:
                continue

            def body(e, lst=lst):
                for th in lst:
                    th(e)
            dec(body)
            self.lists[key] = []


class Ctx:
    pass


_UID = [0]


def _tile(stack, nc, name, shape, dt):
    _UID[0] += 1
    t = stack.enter_context(nc.sbuf_tensor("%s_%d" % (name, _UID[0]), list(shape), dt))
    return t


def _psum(stack, nc, name, shape, dt):
    _UID[0] += 1
    return stack.enter_context(nc.psum_tensor("%s_%d" % (name, _UID[0]), list(shape), dt))


def emit_norm_T(C, src, src_res, gcols, k_list, dstT, dst_res, col0, pbanks, width=D, g_res=None):
    S = C.S
    nk = width // 128
    ss, rs, xs, junk = C.n_ss, C.n_rs, C.n_xs, C.n_junk
    S.op("act", lambda e: e.activation(out=junk[:, 0:width], in_=src[:, 0:width], func=AF.Square,
                                       accum_out=ss[:, 0:1]),
         reads=[src_res], writes=[C.r_junk, C.r_ss])
    S.op("dve", lambda e: e.tensor_scalar(out=rs[:, 0:1], in0=ss[:, 0:1], scalar1=1.0 / width, scalar2=EPS,
                                          op0=ALU.mult, op1=ALU.add), reads=[C.r_ss], writes=[C.r_rs])
    S.op("act", lambda e: e.activation(out=rs[:, 0:1], in_=rs[:, 0:1], func=AF.Sqrt),
         reads=[C.r_rs], writes=[C.r_rs])
    S.op("dve", lambda e: e.reciprocal(out=rs[:, 0:1], in_=rs[:, 0:1]), reads=[C.r_rs], writes=[C.r_rs])
    S.op("dve", lambda e: e.tensor_scalar(out=xs[:, 0:width], in0=src[:, 0:width], scalar1=rs[:, 0:1],
                                          scalar2=None, op0=ALU.mult),
         reads=[src_res, C.r_rs], writes=[C.r_xs])
    import os
    for kq in range(int(os.environ.get('NKQ', nk // 4))):
        pb, pr = pbanks[kq % len(pbanks)]

        def tr(e, kq=kq, pb=pb):
            ins = None
            for a in range(4):
                k = kq * 4 + a
                ins = e.transpose(out=pb[:, a * 128:(a + 1) * 128], in_=xs[:, k * 128:(k + 1) * 128],
                                  identity=C.ident_f[:])
            return ins
        S.op("pe", tr, reads=[C.r_xs], writes=[pr])
        for a in range(4):
            k = kq * 4 + a
            kk = k_list[k]
            if kq % 2 == 0:
                S.op("act", lambda e, a=a, kk=kk, k=k, pb=pb: e.activation(
                    out=dstT[:, kk, col0:col0 + 128], in_=pb[:, a * 128:(a + 1) * 128], func=AF.Copy,
                    scale=gcols[:, k:k + 1]), reads=[pr] + ([g_res] if g_res else []), pwrites=[dst_res])
            else:
                S.op("dve", lambda e, a=a, kk=kk, k=k, pb=pb: e.tensor_scalar(
                    out=dstT[:, kk, col0:col0 + 128], in0=pb[:, a * 128:(a + 1) * 128],
                    scalar1=gcols[:, k:k + 1], scalar2=None, op0=ALU.mult), reads=[pr] + ([g_res] if g_res else []), pwrites=[dst_res])


def emit_ffn(C, nc, src_d, dst_d, gain_d, Wg, Wu, Wd, T, final=None):
    S = C.S
    NG = T // 512
    with ExitStack() as st:
        xt = [_tile(st, nc, "f_xt%d" % i, [128, D], F32) for i in range(2)]
        r_xt = [Res() for _ in range(2)]
        C.n_xs = _tile(st, nc, "f_xs", [128, D], F32); C.r_xs = Res()
        C.n_junk = _tile(st, nc, "f_junk", [128, D], BF16); C.r_junk = Res()
        C.n_ss = _tile(st, nc, "f_ss", [128, 1], F32); C.r_ss = Res()
        C.n_rs = _tile(st, nc, "f_rs", [128, 1], F32); C.r_rs = Res()
        gcols = _tile(st, nc, "f_gc", [128, KD], F32); r_gc = Res()
        xnT = _tile(st, nc, "f_xnT", [128, KD, 512], BF16); r_xnT = Res()
        hT = _tile(st, nc, "f_hT", [128, NFF, 512], BF16); r_hT = Res()
        wg = [_tile(st, nc, "f_wg%d" % i, [128, KD, 256], BF16) for i in range(2)]
        wu = [_tile(st, nc, "f_wu%d" % i, [128, KD, 256], BF16) for i in range(2)]
        r_wg = [Res() for _ in range(2)]; r_wu = [Res() for _ in range(2)]
        wd = [_tile(st, nc, "f_wd%d" % i, [128, 4, 1024], BF16) for i in range(2)]
        r_wd = [Res() for _ in range(2)]
        sg = [_tile(st, nc, "f_sg%d" % i, [128, 512], F32) for i in range(2)]
        r_sg = [Res() for _ in range(2)]
        xh = [_tile(st, nc, "f_xh%d" % i, [128, 1024], F32) for i in range(2)]
        r_xh = [Res() for _ in range(2)]
        ho = [_tile(st, nc, "f_ho%d" % i, [128, 1024], F32) for i in range(2)]
        r_ho = [Res() for _ in range(2)]
        pb = [_psum(st, nc, "f_pb%d" % i, [128, 512], F32) for i in range(8)]
        r_pb = [Res(excl=True) for _ in range(8)]
        r_dst = Res()

        S.op("sp", lambda e: e.dma_start(out=gcols[:], in_=gain_d),
             writes=[r_gc], dma=True)
        nxt = 0
        for g in range(NG):
            t0 = g * 512
            for i in range(4):
                b = nxt % 2; nxt += 1
                r0 = t0 + i * 128
                S.op("sp", lambda e, b=b, r0=r0: e.dma_start(out=xt[b][:], in_=src_d[r0:r0 + 128, :]),
                     writes=[r_xt[b]], dma=True)
                emit_norm_T(C, xt[b], r_xt[b], gcols, list(range(KD)), xnT, r_xnT, i * 128,
                            [(pb[4], r_pb[4]), (pb[5], r_pb[5])], g_res=r_gc)
            for s in range(NFF // 2):
                b = s % 2
                c0 = s * 256
                S.op("pool", lambda e, b=b, c0=c0: e.dma_start(
                    out=wg[b][:], in_=Wg[:, c0:c0 + 256].rearrange("(k p) c -> p k c", p=128)),
                    writes=[r_wg[b]], dma=True)
                S.op("pool", lambda e, b=b, c0=c0: e.dma_start(
                    out=wu[b][:], in_=Wu[:, c0:c0 + 256].rearrange("(k p) c -> p k c", p=128)),
                    writes=[r_wu[b]], dma=True)
                for jj in range(2):
                    j = s * 2 + jj
                    pg, pu = pb[(j % 2) * 2], pb[(j % 2) * 2 + 1]
                    rg, ru = r_pb[(j % 2) * 2], r_pb[(j % 2) * 2 + 1]

                    def mm(e, w=wg[b], p=pg, jj=jj):
                        ins = None
                        for k in range(KD):
                            ins = e.matmul(p[:], lhsT=w[:, k, jj * 128:(jj + 1) * 128], rhs=xnT[:, k, :],
                                           start=(k == 0), stop=(k == KD - 1))
                        return ins
                    S.op("pe", mm, reads=[r_wg[b], r_xnT], writes=[rg])
                    S.op("pe", lambda e, w=wu[b], p=pu, jj=jj: mm(e, w, p, jj), reads=[r_wu[b], r_xnT], writes=[ru])
                    sb = j % 2
                    S.op("act", lambda e, sb=sb, pg=pg: e.activation(out=sg[sb][:], in_=pg[:], func=AF.Silu),
                         reads=[rg], writes=[r_sg[sb]])
                    S.op("dve", lambda e, sb=sb, pu=pu, j=j: e.tensor_tensor(
                        out=hT[:, j, :], in0=sg[sb][:], in1=pu[:], op=ALU.mult),
                        reads=[r_sg[sb], ru], pwrites=[r_hT])
            for hf in range(2):
                for j0 in range(0, NFF, 4):
                    b = (j0 // 4) % 2
                    S.op("pool", lambda e, b=b, j0=j0, hf=hf: e.dma_start(
                        out=wd[b][:], in_=Wd[j0 * 128:(j0 + 4) * 128, hf * 1024:(hf + 1) * 1024].rearrange(
                            "(j p) c -> p j c", p=128)), writes=[r_wd[b]], dma=True)
                    for jj in range(4):
                        j = j0 + jj

                        def dn(e, b=b, jj=jj, j=j):
                            ins = None
                            for i in range(4):
                                for dd in range(2):
                                    ins = e.matmul(pb[i * 2 + dd][:], lhsT=hT[:, j, i * 128:(i + 1) * 128],
                                                   rhs=wd[b][:, jj, dd * 512:(dd + 1) * 512],
                                                   start=(j == 0), stop=(j == NFF - 1))
                            return ins
                        if j == 0:
                            S.op("pe", dn, reads=[r_wd[b], r_hT], writes=r_pb)
                        else:
                            S.op("pe", dn, reads=[r_wd[b], r_hT], pwrites=r_pb)
                for i in range(4):
                    b = nxt % 2; nxt += 1
                    r0 = t0 + i * 128
                    S.op("sp", lambda e, b=b, r0=r0, hf=hf: e.dma_start(
                        out=xh[b][:], in_=src_d[r0:r0 + 128, hf * 1024:(hf + 1) * 1024]),
                        writes=[r_xh[b]], dma=True)
                    for dd in range(2):
                        S.op("dve", lambda e, b=b, i=i, dd=dd: e.scalar_tensor_tensor(
                            out=ho[b][:, dd * 512:(dd + 1) * 512], in0=pb[i * 2 + dd][:], scalar=0.5,
                            in1=xh[b][:, dd * 512:(dd + 1) * 512], op0=ALU.mult, op1=ALU.add),
                            reads=[r_pb[i * 2 + dd], r_xh[b]], pwrites=[r_ho[b]])
                    S.op("sp", lambda e, b=b, r0=r0, hf=hf: e.dma_start(
                        out=dst_d[r0:r0 + 128, hf * 1024:(hf + 1) * 1024], in_=ho[b][:]),
                        reads=[r_ho[b]], pwrites=[r_dst], dma=True)
        S.barrier()
        S.flush()
    return r_dst


def region_sizes(TP, TS):
    A, NA, B = {}, {}, {}
    for part, Tp in (("P", TP), ("S", TS)):
        A["qkv" + part] = (0, 8 * 3 * 128 * Tp)
        A["ab" + part] = (8 * 3 * 128 * Tp, 32 * Tp)
        NA[part] = (8 * 3 * 128 + 32) * Tp
    off = 0
    for name, n in (("szP", 8 * 128 * TP), ("szS", 8 * 128 * TS), ("kTP", 2 * 128 * TP), ("kTS", 2 * 128 * TS),
                    ("vP", TP * 256), ("vS", TS * 256)):
        B[name] = (off, n); off += n
    NB = off
    return A, NA, B, NB


def flat(ap2d):
    return ap2d.rearrange("a b -> (a b)")


def emit_proj(C, nc, h1_d, gain_d, w_in, XA_x, XB_x, qT_d, rope_d, vecs_d, T, TP, TS):
    S = C.S
    NG = T // 512
    RA, NA, RB, NB = region_sizes(TP, TS)
    fa, fb = {p: flat(XA_x[p]) for p in XA_x}, flat(XB_x)

    def regA(name, pattern, **kw):
        o, n = RA[name]
        return fa[name[-1]][o:o + n].rearrange(pattern, **kw)

    def regB(name, pattern, **kw):
        o, n = RB[name]
        return fb[o:o + n].rearrange(pattern, **kw)
    qkv_v = {"P": regA("qkvP", "(h j f t) -> h j f t", h=8, j=3, f=128), "S": regA("qkvS", "(h j f t) -> h j f t", h=8, j=3, f=128)}
    ab_v = {"P": regA("abP", "(r t) -> r t", r=32), "S": regA("abS", "(r t) -> r t", r=32)}
    sz_v = {"P": regB("szP", "(h f t) -> h f t", h=8, f=128), "S": regB("szS", "(h f t) -> h f t", h=8, f=128)}
    kT_v = {"P": regB("kTP", "(h f t) -> f h t", h=2, f=128), "S": regB("kTS", "(h f t) -> f h t", h=2, f=128)}
    v_v = {"P": regB("vP", "(t c) -> t c", c=256), "S": regB("vS", "(t c) -> t c", c=256)}
    qT_v = qT_d.rearrange("h f t -> f h t")
    with ExitStack() as st:
        xt = [_tile(st, nc, "p_xt%d" % i, [128, D], F32) for i in range(2)]
        r_xt = [Res() for _ in range(2)]
        C.n_xs = _tile(st, nc, "p_xs", [128, D], F32); C.r_xs = Res()
        C.n_junk = _tile(st, nc, "p_junk", [128, D], BF16); C.r_junk = Res()
        C.n_ss = _tile(st, nc, "p_ss", [128, 1], F32); C.r_ss = Res()
        C.n_rs = _tile(st, nc, "p_rs", [128, 1], F32); C.r_rs = Res()
        gcols = _tile(st, nc, "p_gc", [128, KD], F32); r_gc = Res()
        uT = _tile(st, nc, "p_uT", [128, KD, 512], BF16); r_uT = Res()
        ws = [_tile(st, nc, "p_w%d" % i, [128, KD, 256], BF16) for i in range(3)]
        r_ws = [Res() for _ in range(3)]
        wab = _tile(st, nc, "p_wab", [128, KD, 32], BF16); r_wab = Res()
        stg = [_tile(st, nc, "p_stg%d" % i, [128, 512], F32) for i in range(3)]
        r_stg = [Res() for _ in range(3)]
        stgb = [_tile(st, nc, "p_stgb%d" % i, [128, 512], BF16) for i in range(2)]
        r_stgb = [Res() for _ in range(2)]
        gbc = _tile(st, nc, "p_gbc", [128, 2, 128], F32); r_gbc = Res()
        rope = [_tile(st, nc, "p_rope%d" % i, [128, 2, 2, 32], F32) for i in range(4)]
        r_rope = [Res() for _ in range(4)]
        qs = _tile(st, nc, "p_qs", [128, 2, 128], F32); r_qs = Res()
        sq = _tile(st, nc, "p_sq", [128, 2, 128], F32); r_sq = Res()
        ss2 = _tile(st, nc, "p_ss2", [128, 2], F32); r_ss2 = Res()
        qn = _tile(st, nc, "p_qn", [128, 2, 2, 2, 32], F32); r_qn = Res()
        ta = _tile(st, nc, "p_ta", [128, 2, 32], F32); r_ta = Res()
        tb = _tile(st, nc, "p_tb", [128, 2, 32], F32); r_tb = Res()
        qr = [_tile(st, nc, "p_qr%d" % i, [128, 2, 2, 2, 32], BF16) for i in range(2)]
        r_qr = [Res() for _ in range(2)]
        vst = [_tile(st, nc, "p_vst%d" % i, [128, 256], BF16) for i in range(2)]
        r_vst = [Res() for _ in range(2)]
        QTs = _tile(st, nc, "p_QTs", [128, 8, 512], BF16); r_QTs = Res()
        KTs = _tile(st, nc, "p_KTs", [128, 2, 512], BF16); r_KTs = Res()
        identb = C.ident_b
        pT = [_psum(st, nc, "p_pT%d" % i, [128, 512], F32) for i in range(2)]
        pf = [_psum(st, nc, "p_pf%d" % i, [128, 512], F32) for i in range(2)]
        pa = [_psum(st, nc, "p_pa%d" % i, [128, 512], F32) for i in range(2)]
        ptr = [_psum(st, nc, "p_ptr%d" % i, [128, 1024], BF16) for i in range(2)]
        r_pT = [Res(excl=True) for _ in range(2)]; r_pf = [Res(excl=True) for _ in range(2)]
        r_pa = [Res(excl=True) for _ in range(2)]; r_ptr = [Res(excl=True) for _ in range(2)]
        r_out = C.r_xch

        S.op("sp", lambda e: e.dma_start(out=gcols[:], in_=gain_d), writes=[r_gc], dma=True)
        for i in range(2):
            S.op("sp", lambda e, i=i: e.dma_start(out=gbc[:, i, :], in_=vecs_d[i].partition_broadcast(128)),
                 pwrites=[r_gbc], dma=True)
        nxt = 0
        wrr = 0
        cnt_pa = 0
        for g in range(NG):
            t0 = g * 512
            part = "P" if t0 < TP else "S"
            o = t0 if part == "P" else t0 - TP
            for i in range(4):
                b = nxt % 2; nxt += 1
                r0 = t0 + i * 128
                S.op("sp", lambda e, b=b, r0=r0: e.dma_start(out=xt[b][:], in_=h1_d[r0:r0 + 128, :]),
                     writes=[r_xt[b]], dma=True)
                emit_norm_T(C, xt[b], r_xt[b], gcols, list(range(KD)), uT, r_uT, i * 128,
                            [(pT[0], r_pT[0]), (pT[1], r_pT[1])], g_res=r_gc)
                S.op("sp", lambda e, i=i, r0=r0: e.dma_start(
                    out=rope[i][:].rearrange("p a b c -> p (a b c)"), in_=rope_d[r0:r0 + 128, :]),
                    writes=[r_rope[i]], dma=True)
            for cs in range(16):
                wb = wrr % 3; wrr += 1
                S.op("pool", lambda e, wb=wb, cs=cs: e.dma_start(
                    out=ws[wb][:], in_=w_in[:, cs * 256:(cs + 1) * 256].rearrange("(k p) c -> p k c", p=128)),
                    writes=[r_ws[wb]], dma=True)
                for jj in range(2):
                    cc = cs * 2 + jj
                    j, h = cc // 8, cc % 8
                    p, rp = pf[cc % 2], r_pf[cc % 2]

                    def mm(e, wb=wb, jj=jj, p=p):
                        ins = None
                        for k in range(KD):
                            ins = e.matmul(p[:], lhsT=ws[wb][:, k, jj * 128:(jj + 1) * 128], rhs=uT[:, k, :],
                                           start=(k == 0), stop=(k == KD - 1))
                        return ins
                    S.op("pe", mm, reads=[r_ws[wb], r_uT], writes=[rp])
                    if j < 3:
                        sb = cc % 3
                        if cc % 2 == 0:
                            S.op("act", lambda e, sb=sb, p=p: e.activation(out=stg[sb][:], in_=p[:], func=AF.Copy),
                                 reads=[rp], writes=[r_stg[sb]])
                        else:
                            S.op("dve", lambda e, sb=sb, p=p: e.tensor_copy(out=stg[sb][:], in_=p[:]),
                                 reads=[rp], writes=[r_stg[sb]])
                        S.op("sp", lambda e, sb=sb, h=h, j=j, part=part, o=o: e.dma_start(
                            out=qkv_v[part][h, j, :, o:o + 512], in_=stg[sb][:]),
                            reads=[r_stg[sb]], pwrites=[r_out], dma=True)
                    else:
                        sb = cc % 2
                        S.op("act", lambda e, sb=sb, p=p: e.activation(out=stgb[sb][:], in_=p[:], func=AF.Silu),
                             reads=[rp], writes=[r_stgb[sb]])
                        S.op("sp", lambda e, sb=sb, h=h, part=part, o=o: e.dma_start(
                            out=sz_v[part][h, :, o:o + 512], in_=stgb[sb][:]),
                            reads=[r_stgb[sb]], pwrites=[r_out], dma=True)
            S.op("pool", lambda e: e.dma_start(
                out=wab[:], in_=w_in[:, 4096:4128].rearrange("(k p) c -> p k c", p=128)),
                writes=[r_wab], dma=True)

            def mmab(e):
                ins = None
                for k in range(KD):
                    ins = e.matmul(pf[0][0:32, :], lhsT=wab[:, k, 0:32], rhs=uT[:, k, :],
                                   start=(k == 0), stop=(k == KD - 1))
                return ins
            S.op("pe", mmab, reads=[r_wab, r_uT], writes=[r_pf[0]])
            S.op("dve", lambda e: e.tensor_copy(out=stg[0][0:32, :], in_=pf[0][0:32, :]),
                 reads=[r_pf[0]], writes=[r_stg[0]])
            S.op("sp", lambda e, part=part, o=o: e.dma_start(out=ab_v[part][:, o:o + 512], in_=stg[0][0:32, :]),
                 reads=[r_stg[0]], pwrites=[r_out], dma=True)
            for s6 in range(6):
                wb = wrr % 3; wrr += 1
                c0 = 4128 + s6 * 256
                S.op("pool", lambda e, wb=wb, c0=c0: e.dma_start(
                    out=ws[wb][:], in_=w_in[:, c0:c0 + 256].rearrange("(k p) c -> p k c", p=128)),
                    writes=[r_ws[wb]], dma=True)
                for i in range(4):
                    pi = cnt_pa % 2; cnt_pa += 1
                    p, rp = pa[pi], r_pa[pi]

                    def mma(e, wb=wb, i=i, p=p):
                        ins = None
                        for k in range(KD):
                            ins = e.matmul(p[:, 0:256], lhsT=uT[:, k, i * 128:(i + 1) * 128], rhs=ws[wb][:, k, :],
                                           start=(k == 0), stop=(k == KD - 1))
                        return ins
                    S.op("pe", mma, reads=[r_ws[wb], r_uT], writes=[rp])
                    if s6 == 5:
                        vb = i % 2
                        S.op("act", lambda e, vb=vb, p=p: e.activation(out=vst[vb][:], in_=p[:, 0:256], func=AF.Copy),
                             reads=[rp], writes=[r_vst[vb]])
                        S.op("sp", lambda e, vb=vb, part=part, o=o, i=i: e.dma_start(
                            out=v_v[part][o + i * 128:o + (i + 1) * 128, :], in_=vst[vb][:]),
                            reads=[r_vst[vb]], pwrites=[r_out], dma=True)
                        continue
                    gi = 0 if s6 < 4 else 1
                    S.op("act", lambda e, p=p: e.activation(out=qs[:].rearrange("p a b -> p (a b)"), in_=p[:, 0:256],
                                                            func=AF.Copy), reads=[rp], writes=[r_qs])
                    S.op("dve", lambda e: e.tensor_tensor(out=sq[:], in0=qs[:], in1=qs[:], op=ALU.mult),
                         reads=[r_qs], writes=[r_sq])
                    S.op("dve", lambda e: e.tensor_reduce(out=ss2[:], in_=sq[:], axis=AX.X, op=ALU.add),
                         reads=[r_sq], writes=[r_ss2])
                    S.op("dve", lambda e: e.tensor_scalar(out=ss2[:], in0=ss2[:], scalar1=1.0 / 128, scalar2=EPS,
                                                          op0=ALU.mult, op1=ALU.add), reads=[r_ss2], writes=[r_ss2])
                    S.op("act", lambda e: e.activation(out=ss2[:], in_=ss2[:], func=AF.Sqrt),
                         reads=[r_ss2], writes=[r_ss2])
                    S.op("dve", lambda e: e.reciprocal(out=ss2[:], in_=ss2[:]), reads=[r_ss2], writes=[r_ss2])
                    qb = cnt_pa % 2
                    for hh in range(2):
                        S.op("dve", lambda e, hh=hh, gi=gi: e.scalar_tensor_tensor(
                            out=qn[:, hh].rearrange("p a b c -> p (a b c)"), in0=qs[:, hh, :], scalar=ss2[:, hh:hh + 1],
                            in1=gbc[:, gi, :], op0=ALU.mult, op1=ALU.mult),
                            reads=[r_qs, r_ss2, r_gbc], pwrites=[r_qn])
                    for hh in range(2):
                        x1 = qn[:, hh, :, 0, :]
                        x2 = qn[:, hh, :, 1, :]
                        Cc = rope[i][:, 0]
                        Sn = rope[i][:, 1]
                        S.op("dve", lambda e, x1=x1, Cc=Cc: e.tensor_tensor(out=ta[:], in0=x1, in1=Cc, op=ALU.mult),
                             reads=[r_qn, r_rope[i]], writes=[r_ta])
                        S.op("dve", lambda e, x2=x2, Sn=Sn: e.tensor_tensor(out=tb[:], in0=x2, in1=Sn, op=ALU.mult),
                             reads=[r_qn, r_rope[i]], writes=[r_tb])
                        S.op("dve", lambda e, hh=hh, qb=qb: e.tensor_tensor(out=qr[qb][:, hh, :, 0, :], in0=ta[:], in1=tb[:],
                                                                            op=ALU.subtract),
                             reads=[r_ta, r_tb], pwrites=[r_qr[qb]])
                        S.op("dve", lambda e, x2=x2, Cc=Cc: e.tensor_tensor(out=ta[:], in0=x2, in1=Cc, op=ALU.mult),
                             reads=[r_qn, r_rope[i]], writes=[r_ta])
                        S.op("dve", lambda e, x1=x1, Sn=Sn: e.tensor_tensor(out=tb[:], in0=x1, in1=Sn, op=ALU.mult),
                             reads=[r_qn, r_rope[i]], writes=[r_tb])
                        S.op("dve", lambda e, hh=hh, qb=qb: e.tensor_tensor(out=qr[qb][:, hh, :, 1, :], in0=ta[:], in1=tb[:],
                                                                            op=ALU.add),
                             reads=[r_ta, r_tb], pwrites=[r_qr[qb]])
                    pt, rpt = ptr[qb], r_ptr[qb]

                    def trq(e, qb=qb, pt=pt):
                        ins = None
                        for hh in range(2):
                            ins = e.transpose(out=pt[:, hh * 128:(hh + 1) * 128],
                                              in_=qr[qb][:, hh].rearrange("p a b c -> p (a b c)"), identity=identb[:])
                        return ins
                    S.op("pe", trq, reads=[r_qr[qb]], writes=[rpt])
                    if s6 < 4:
                        S.op("act", lambda e, pt=pt, s6=s6, i=i: e.activation(
                            out=QTs[:, 2 * s6:2 * s6 + 2, i * 128:(i + 1) * 128],
                            in_=pt[:, 0:256].rearrange("p (a t) -> p a t", a=2), func=AF.Copy),
                            reads=[rpt], pwrites=[r_QTs])
                    else:
                        S.op("act", lambda e, pt=pt, i=i: e.activation(
                            out=KTs[:, :, i * 128:(i + 1) * 128],
                            in_=pt[:, 0:256].rearrange("p (a t) -> p a t", a=2), func=AF.Copy),
                            reads=[rpt], pwrites=[r_KTs])
                    r_qr[qb].r = dict(r_qr[qb].r)
            S.op("sp", lambda e, t0=t0: e.dma_start(out=qT_v[:, :, t0:t0 + 512], in_=QTs[:]),
                 reads=[r_QTs], pwrites=[C.r_qT], dma=True)
            S.op("sp", lambda e, part=part, o=o: e.dma_start(out=kT_v[part][:, :, o:o + 512], in_=KTs[:]),
                 reads=[r_KTs], pwrites=[r_out], dma=True)
        S.barrier()
        S.flush()


def emit_attn(C, nc, XB_g, qT_d, mixT_d, vecs_d, T, TP, TS):
    S = C.S
    RA, NA, RB, NB = region_sizes(TP, TS)
    fbg = flat(XB_g)
    SCALE = float(HD) ** -0.5
    LqM = max(TP, TS)
    with ExitStack() as st:
        KT = _tile(st, nc, "a_KT", [128, 8 * LqM], BF16); r_KT = Res()
        V1 = _tile(st, nc, "a_V1", [128, 8 * LqM // 128, 130], BF16); r_V1 = Res()
        QB = [_tile(st, nc, "a_QB%d" % i, [128, 4, 128], BF16) for i in range(2)]
        r_QB = [Res() for _ in range(2)]
        PT = [_tile(st, nc, "a_PT%d" % i, [128, 512], BF16) for i in range(3)]
        r_PT = [Res() for _ in range(3)]
        oT = _tile(st, nc, "a_oT", [128, 4, LqM], BF16); r_oT = Res()
        ot = _tile(st, nc, "a_ot", [128, 4, 128], F32); r_ot = Res()
        sq = _tile(st, nc, "a_sq", [128, 4, 128], F32); r_sq = Res()
        onb = _tile(st, nc, "a_onb", [128, 4, 128], BF16); r_onb = Res()
        rec = _tile(st, nc, "a_rec", [128, 4], F32); r_rec = Res()
        ss4 = _tile(st, nc, "a_ss4", [128, 4], F32); r_ss4 = Res()
        gbc = _tile(st, nc, "a_gbc", [128, 128], F32); r_gbc = Res()
        vq = _tile(st, nc, "a_vq", [1, 2, 128], F32); r_vq = Res()
        m2 = _tile(st, nc, "a_m2", [1, 2], F32); r_m2 = Res()
        ones_r = _tile(st, nc, "a_ones", [1, 128], F32); r_ones = Res()
        negM = _tile(st, nc, "a_negM", [128, 1], F32); r_negM = Res()
        ps = [_psum(st, nc, "a_ps%d" % i, [128, 512], F32) for i in range(3)]
        po = [_psum(st, nc, "a_po%d" % i, [128, 512], F32) for i in range(4)]
        ptr = _psum(st, nc, "a_ptr", [128, 1024], BF16)
        r_ps = [Res(excl=True) for _ in range(3)]; r_po = [Res(excl=True) for _ in range(4)]
        r_ptr = Res(excl=True)

        S.op("sp", lambda e: e.dma_start(out=vq[:], in_=vecs_d[0:2, :].rearrange("(o a) d -> o a d", o=1)),
             writes=[r_vq], dma=True)
        S.op("sp", lambda e: e.dma_start(out=gbc[:], in_=vecs_d[2].partition_broadcast(128)), writes=[r_gbc], dma=True)
        S.op("dve", lambda e: e.memset(ones_r[:], 1.0), writes=[r_ones])
        S.op("dve", lambda e: e.memset(V1[:], 1.0), writes=[r_V1])
        S.op("dve", lambda e: e.tensor_reduce(out=m2[:], in_=vq[:], axis=AX.X, op=ALU.max, apply_absolute_value=True),
             reads=[r_vq], writes=[r_m2])
        S.op("dve", lambda e: e.tensor_tensor(out=m2[:, 0:1], in0=m2[:, 0:1], in1=m2[:, 1:2], op=ALU.mult),
             reads=[r_m2], writes=[r_m2])
        S.op("dve", lambda e: e.tensor_scalar(out=m2[:, 0:1], in0=m2[:, 0:1], scalar1=-(float(HD) ** 0.5), scalar2=None,
                                              op0=ALU.mult), reads=[r_m2], writes=[r_m2])
        S.op("pe", lambda e: e.matmul(ps[0][:, 0:1], lhsT=ones_r[:], rhs=m2[:, 0:1], start=True, stop=True),
             reads=[r_ones, r_m2], writes=[r_ps[0]])
        S.op("dve", lambda e: e.tensor_copy(out=negM[:], in_=ps[0][:, 0:1]), reads=[r_ps[0]], writes=[r_negM])

        cnt = 0
        qcnt = 0
        for part, Lq, lt0 in (("P", TP, 0), ("S", TS, TP)):
            nkb = 8 * Lq // 128
            okT, _ = RB["kT" + part]
            ov, _ = RB["v" + part]
            for kvh in range(2):
                for r in range(NCORE):
                    kreg = fbg[r * NB + okT:r * NB + okT + 2 * 128 * Lq].rearrange("(h f t) -> h f t", h=2, f=128)
                    vreg = fbg[r * NB + ov:r * NB + ov + Lq * 256].rearrange("(t c) -> t c", c=256)
                    S.op("sp", lambda e, r=r, kreg=kreg, kvh=kvh, Lq=Lq: e.dma_start(
                        out=KT[:, r * Lq:(r + 1) * Lq], in_=kreg[kvh]), reads=[C.r_xchg], pwrites=[r_KT], dma=True)
                    nb = Lq // 128
                    S.op("sp", lambda e, r=r, vreg=vreg, kvh=kvh, nb=nb: e.dma_start(
                        out=V1[:, r * nb:(r + 1) * nb, 0:128],
                        in_=vreg[:, kvh * 128:(kvh + 1) * 128].rearrange("(b p) d -> p b d", p=128)),
                        reads=[C.r_xchg], pwrites=[r_V1], dma=True)
                for qb in range(Lq // 128):
                    q0 = lt0 + qb * 128
                    qi = qcnt % 2; qcnt += 1
                    S.op("sp", lambda e, qi=qi, kvh=kvh, q0=q0: e.dma_start(
                        out=QB[qi][:], in_=qT_d[4 * kvh:4 * kvh + 4, :, q0:q0 + 128].rearrange("h f t -> f h t")),
                        reads=[C.r_qT], writes=[r_QB[qi]], dma=True)
                    LAG = 2
                    bufs = []
                    for kk in range(nkb + LAG):
                        if kk < nkb:
                            kb = kk
                            b = cnt % 3; cnt += 1
                            bufs.append(b)
                            S.op("pe", lambda e, b=b, kb=kb, qi=qi: e.matmul(
                                ps[b][:], lhsT=KT[:, kb * 128:(kb + 1) * 128], rhs=QB[qi][:].rearrange("p a t -> p (a t)"),
                                start=True, stop=True), reads=[r_KT, r_QB[qi]], writes=[r_ps[b]])
                            S.op("act", lambda e, b=b: e.activation(out=PT[b][:], in_=ps[b][:], func=AF.Exp,
                                                                    bias=negM[:, 0:1], scale=SCALE),
                                 reads=[r_ps[b], r_negM], writes=[r_PT[b]])
                        if kk >= LAG:
                            kb = kk - LAG
                            b = bufs[kb]

                            def pv(e, b=b, kb=kb, nkb=nkb):
                                ins = None
                                for hh in range(4):
                                    ins = e.matmul(po[hh][:, 0:129], lhsT=PT[b][:, hh * 128:(hh + 1) * 128],
                                                   rhs=V1[:, kb, 0:129], start=(kb == 0), stop=(kb == nkb - 1))
                                return ins
                            if kb == 0:
                                S.op("pe", pv, reads=[r_PT[b], r_V1], writes=r_po)
                            else:
                                S.op("pe", pv, reads=[r_PT[b], r_V1], pwrites=r_po)
                    for hh in range(4):
                        S.op("dve", lambda e, hh=hh: e.reciprocal(out=rec[:, hh:hh + 1], in_=po[hh][:, 128:129]),
                             reads=[r_po[hh]], pwrites=[r_rec])
                        S.op("dve", lambda e, hh=hh: e.tensor_scalar(out=ot[:, hh, :], in0=po[hh][:, 0:128],
                                                                     scalar1=rec[:, hh:hh + 1], scalar2=None, op0=ALU.mult),
                             reads=[r_po[hh], r_rec], pwrites=[r_ot])
                    S.op("dve", lambda e: e.tensor_tensor(out=sq[:], in0=ot[:], in1=ot[:], op=ALU.mult),
                         reads=[r_ot], writes=[r_sq])
                    S.op("dve", lambda e: e.tensor_reduce(out=ss4[:], in_=sq[:], axis=AX.X, op=ALU.add),
                         reads=[r_sq], writes=[r_ss4])
                    S.op("dve", lambda e: e.tensor_scalar(out=ss4[:], in0=ss4[:], scalar1=1.0 / 128, scalar2=EPS,
                                                          op0=ALU.mult, op1=ALU.add), reads=[r_ss4], writes=[r_ss4])
                    S.op("act", lambda e: e.activation(out=ss4[:], in_=ss4[:], func=AF.Sqrt), reads=[r_ss4], writes=[r_ss4])
                    S.op("dve", lambda e: e.reciprocal(out=ss4[:], in_=ss4[:]), reads=[r_ss4], writes=[r_ss4])
                    for hh in range(4):
                        S.op("dve", lambda e, hh=hh: e.scalar_tensor_tensor(
                            out=onb[:, hh, :], in0=ot[:, hh, :], scalar=ss4[:, hh:hh + 1], in1=gbc[:],
                            op0=ALU.mult, op1=ALU.mult), reads=[r_ot, r_ss4, r_gbc], pwrites=[r_onb])

                    def trs(e):
                        ins = None
                        for hh in range(4):
                            ins = e.transpose(out=ptr[:, hh * 128:(hh + 1) * 128], in_=onb[:, hh, :], identity=C.ident_b[:])
                        return ins
                    S.op("pe", trs, reads=[r_onb], writes=[r_ptr])
                    S.op("act", lambda e, qb=qb: e.activation(out=oT[:, :, qb * 128:(qb + 1) * 128],
                                                              in_=ptr[:, 0:512].rearrange("p (a t) -> p a t", a=4), func=AF.Copy),
                         reads=[r_ptr], pwrites=[r_oT])
                S.op("sp", lambda e, kvh=kvh, lt0=lt0, Lq=Lq: e.dma_start(
                    out=mixT_d[4 * kvh:4 * kvh + 4, :, lt0:lt0 + Lq].rearrange("h f t -> f h t"), in_=oT[:, :, 0:Lq]),
                    reads=[r_oT], pwrites=[C.r_mixA], dma=True)
        S.barrier()
        S.flush()


def idx_layout(LP, LS):
    keys = []
    for part in ("P", "S"):
        for r in range(NCORE):
            for j in range(3):
                keys += [("q", part, r, j, "m"), ("q", part, r, j, "l"), ("q", part, r, j, "r")]
            keys += [("sz", part, r), ("a", part, r), ("b", part, r)]
    TP, TS = LP // NCORE, LS // NCORE
    for g in range((TP + TS) // 512):
        for h in range(8):
            keys.append(("og", g, h))
    return {k: i for i, k in enumerate(keys)}


def og_sizes(LP, LS):
    return {"P": (0, 128 * LP), "S": (128 * LP, 128 * LS)}, 128 * (LP + LS)


def emit_gdn(C, nc, XA_g, XB_g, OG_x, of_d, idx_t, r_idx, hv_d, vecs_d, convw_d, masks_d, LP, LS):
    S = C.S
    NB = 4
    TP, TS = LP // NCORE, LS // NCORE
    RA, NA, RB, NB_ = region_sizes(TP, TS)
    fagd, fbg = {p: flat(XA_g[p]) for p in XA_g}, flat(XB_g)
    KI = idx_layout(LP, LS)
    OGR, NO = og_sizes(LP, LS)
    fog = flat(OG_x)
    SEGM = max(TP, TS)
    W = NB * 128
    with ExitStack() as st:
        T_ = lambda name, shape, dt=F32: _tile(st, nc, "g_" + name, shape, dt)
        raw = [T_("raw%d" % j, [128, SEGM + 4]) for j in range(3)]; r_raw = [Res() for _ in range(3)]
        cv = [T_("cv%d" % j, [128, SEGM]) for j in range(3)]; r_cv = [Res() for _ in range(3)]
        sqs = T_("sqs", [128, 512]); r_sqs = Res()
        rn = T_("rn", [128, 512]); r_rn = Res()
        qTb = T_("qTb", [128, SEGM], BF16); kTb = T_("kTb", [128, SEGM], BF16)
        szs = [T_("szs%d" % i, [128, SEGM], BF16) for i in range(2)]; r_szs = [Res() for _ in range(2)]
        ogs = [T_("ogs%d" % i, [128, SEGM], BF16) for i in range(2)]; r_ogs = [Res() for _ in range(2)]
        a2 = T_("a2", [2, SEGM]); b2 = T_("b2", [2, SEGM]); r_a2 = Res(); r_b2 = Res()
        t2a = T_("t2a", [2, SEGM]); t2b = T_("t2b", [2, SEGM]); r_t2a = Res(); r_t2b = Res()
        hv = T_("hv", [2, 4]); r_hv = Res()
        cw = T_("cw", [128, 3, 5]); r_cw = Res()
        gbc = T_("gbc", [128, 128]); r_gbc = Res()
        masks = T_("masks", [128, 7, 128]); r_masks = Res()
        ones_f = masks[:, 4, :]
        identf = C.ident_f
        gbt = T_("gbt", [128, NB, 4]); r_gbt = Res()
        g_bc = T_("g_bc", [128, NB, 128]); r_g_bc = Res()
        cols = T_("cols", [128, NB, 8]); r_cols = Res()
        Dm = T_("Dm", [128, NB, 128]); r_Dm = Res()
        decI = T_("decI", [128, NB, 128]); decS = T_("decS", [128, NB, 128]); r_decI = Res(); r_decS = Res()
        dgb = T_("dgb", [128, NB, 128]); r_dgb = Res()
        XA_ = [T_("XA%d" % i, [128, NB, 128], BF16) for i in range(2)]; r_XA = [Res() for _ in range(2)]
        XT_ = [T_("XT%d" % i, [128, NB, 128], BF16) for i in range(2)]; r_XT = [Res() for _ in range(2)]
        X0f = T_("X0f", [128, NB, 128]); r_X0f = Res()
        gi2 = T_("gi2", [128, NB, 2]); r_gi2 = Res()
        Pm = T_("Pm", [128, NB, 128]); r_Pm = Res()
        Pb = T_("Pb", [128, NB, 128], BF16); r_Pb = Res()
        Egc = T_("Egc", [128, NB, 128]); r_Egc = Res()
        bv = T_("bv", [128, NB, 128], BF16); bke = T_("bke", [128, NB, 128], BF16)
        r_bv = Res(); r_bke = Res()
        AQT = [T_("AQT%d" % i, [128, NB, 128], BF16) for i in range(2)]; r_AQT = [Res() for _ in range(2)]
        qdT = [T_("qdT%d" % i, [128, NB, 128], BF16) for i in range(2)]; r_qdT = [Res() for _ in range(2)]
        kdc = [T_("kdc%d" % i, [128, NB, 128], BF16) for i in range(2)]; r_kdc = [Res() for _ in range(2)]
        u_t = [T_("u%d" % i, [128, NB, 128]) for i in range(2)]; r_u = [Res() for _ in range(2)]
        wT = [T_("wT%d" % i, [128, NB, 128], BF16) for i in range(2)]; r_wT = [Res() for _ in range(2)]
        cdx = [T_("cdx%d" % i, [128, NB, 2]) for i in range(2)]; r_cdx = [Res() for _ in range(2)]
        vn = T_("vn", [128, 128], BF16); r_vn = Res()
        Sst = T_("S", [128, 128]); r_S = Res()
        Sb = T_("Sb", [128, 128], BF16); r_Sb = Res()
        o_t = [T_("o_t%d" % i, [128, 128]) for i in range(2)]; r_o = [Res() for _ in range(2)]
        of_t = T_("of_t", [128, 128]); r_of = Res()
        osq = T_("osq", [128, 128]); r_osq = Res()
        oss = T_("oss", [128, 1]); r_oss = Res()
        on_t = T_("on_t", [128, 128]); r_on = Res()
        Bk = [_psum(st, nc, "g_B%d" % i, [128, 512], F32) for i in range(5)]
        r_B = [Res(excl=True) for _ in range(5)]
        pS = _psum(st, nc, "g_pS", [128, 512], F32); r_pS = Res(excl=True)
        pD = _psum(st, nc, "g_pD", [128, 512], F32); r_pD = Res(excl=True)
        pN = _psum(st, nc, "g_pN", [128, 512], F32); r_pN = Res(excl=True)

        S.op("sp", lambda e: e.dma_start(out=hv[:], in_=hv_d), writes=[r_hv], dma=True)
        S.op("sp", lambda e: e.dma_start(out=cw[:], in_=convw_d), writes=[r_cw], dma=True)
        S.op("sp", lambda e: e.dma_start(out=gbc[:], in_=vecs_d[3].partition_broadcast(128)), writes=[r_gbc], dma=True)
        S.op("sp", lambda e: e.dma_start(out=masks[:], in_=masks_d), writes=[r_masks], dma=True)
        S.op("act", lambda e: e.activation(out=hv[:, 2:3], in_=hv[:, 0:1], func=AF.Exp), reads=[r_hv], writes=[r_hv])
        S.op("dve", lambda e: e.tensor_scalar(out=hv[:, 2:3], in0=hv[:, 2:3], scalar1=-1.0, scalar2=None, op0=ALU.mult),
             reads=[r_hv], writes=[r_hv])

        def gather(out_ap, src_flat, F, key, npart, reads, wres, partial=True):
            col = KI[key]
            view = src_flat.rearrange("(m f) -> m f", f=F)
            kw = dict(pwrites=[wres]) if partial else dict(writes=[wres])
            S.op("pool", lambda e: e.indirect_dma_start(
                out=out_ap, out_offset=None, in_=view,
                in_offset=bass.IndirectOffsetOnAxis(ap=idx_t[0:npart, col:col + 1], axis=0)),
                reads=reads + [r_idx], dma=True, **kw)

        def bcl(ap3):
            return ap3.broadcast_to([128, NB, 128])

        def bcm(ap2):
            return ap2.unsqueeze(1).broadcast_to([128, NB, 128])

        def seg_load(part, SEG, dirn, r, sp):
            fag = fagd[part]
            for j in range(3):
                if r == 0:
                    S.op("dve", lambda e, j=j: e.memset(raw[j][:, 0:2], 0.0), pwrites=[r_raw[j]])
                else:
                    gather(raw[j][:, 0:2], fag, 2, ("q", part, r, j, "l"), 128, [C.r_xchg], r_raw[j])
                if r == NCORE - 1:
                    S.op("dve", lambda e, j=j, SEG=SEG: e.memset(raw[j][:, SEG + 2:SEG + 4], 0.0), pwrites=[r_raw[j]])
                else:
                    gather(raw[j][:, SEG + 2:SEG + 4], fag, 2, ("q", part, r, j, "r"), 128, [C.r_xchg], r_raw[j])
                gather(raw[j][:, 2:SEG + 2], fag, SEG, ("q", part, r, j, "m"), 128, [C.r_xchg], r_raw[j])
                yield
            gather(a2[:, 0:SEG], fag, SEG, ("a", part, r), 2, [C.r_xchg], r_a2, partial=False)
            gather(b2[:, 0:SEG], fag, SEG, ("b", part, r), 2, [C.r_xchg], r_b2, partial=False)
            if dirn == 1:
                gather(szs[sp][:, 0:SEG], fbg, SEG, ("sz", part, r), 128, [C.r_xchg], r_szs[sp], partial=False)
            yield
            for j in range(3):
                S.op("dve", lambda e, j=j, SEG=SEG: e.tensor_scalar(
                    out=cv[j][:, 0:SEG], in0=raw[j][:, 0:SEG], scalar1=cw[:, j, 0:1], scalar2=None, op0=ALU.mult),
                    reads=[r_raw[j], r_cw], writes=[r_cv[j]])
                yield
                for tap in range(1, 5):
                    S.op("dve", lambda e, j=j, SEG=SEG, tap=tap: e.scalar_tensor_tensor(
                        out=cv[j][:, 0:SEG], in0=raw[j][:, tap:tap + SEG], scalar=cw[:, j, tap:tap + 1],
                        in1=cv[j][:, 0:SEG], op0=ALU.mult, op1=ALU.add),
                        reads=[r_raw[j], r_cw, r_cv[j]], writes=[r_cv[j]])
                    yield
                S.op("act", lambda e, j=j, SEG=SEG: e.activation(out=cv[j][:, 0:SEG], in_=cv[j][:, 0:SEG], func=AF.Silu),
                     reads=[r_cv[j]], writes=[r_cv[j]])
                yield
            for j, dst, scl in ((0, qTb, float(HD) ** -0.5), (1, kTb, 1.0)):
                for c0 in range(0, SEG, 512):
                    S.op("dve", lambda e, j=j, c0=c0: e.tensor_tensor(out=sqs[:], in0=cv[j][:, c0:c0 + 512],
                                                                       in1=cv[j][:, c0:c0 + 512], op=ALU.mult),
                         reads=[r_cv[j]], writes=[r_sqs])
                    S.op("pe", lambda e: e.matmul(pN[:], lhsT=ones_f, rhs=sqs[:], start=True, stop=True),
                         reads=[r_sqs, r_masks], writes=[r_pN])
                    S.op("dve", lambda e: e.tensor_scalar(out=rn[:], in0=pN[:], scalar1=EPS, scalar2=None, op0=ALU.add),
                         reads=[r_pN], writes=[r_rn])
                    S.op("act", lambda e: e.activation(out=rn[:], in_=rn[:], func=AF.Sqrt), reads=[r_rn], writes=[r_rn])
                    yield
                    S.op("dve", lambda e: e.reciprocal(out=rn[:], in_=rn[:]), reads=[r_rn], writes=[r_rn])
                    S.op("dve", lambda e, j=j, c0=c0, scl=scl: e.scalar_tensor_tensor(
                        out=cv[j][:, c0:c0 + 512], in0=cv[j][:, c0:c0 + 512], scalar=scl, in1=rn[:],
                        op0=ALU.mult, op1=ALU.mult), reads=[r_rn, r_cv[j]], writes=[r_cv[j]])
                    yield
                S.op("act", lambda e, j=j, dst=dst, SEG=SEG: e.activation(out=dst[:, 0:SEG], in_=cv[j][:, 0:SEG], func=AF.Copy),
                     reads=[r_cv[j]], writes=[r_cv[j]])
                yield
            S.op("dve", lambda e, SEG=SEG: e.tensor_scalar(out=a2[:, 0:SEG], in0=a2[:, 0:SEG], scalar1=hv[:, 1:2], scalar2=None,
                                                           op0=ALU.add), reads=[r_a2, r_hv], writes=[r_a2])
            S.op("act", lambda e, SEG=SEG: e.activation(out=t2a[:, 0:SEG], in_=a2[:, 0:SEG], func=AF.Abs),
                 reads=[r_a2], writes=[r_t2a])
            yield
            S.op("act", lambda e, SEG=SEG: e.activation(out=t2a[:, 0:SEG], in_=t2a[:, 0:SEG], func=AF.Exp, scale=-1.0),
                 reads=[r_t2a], writes=[r_t2a])
            S.op("act", lambda e, SEG=SEG: e.activation(out=t2a[:, 0:SEG], in_=t2a[:, 0:SEG], func=AF.Ln, bias=1.0),
                 reads=[r_t2a], writes=[r_t2a])
            yield
            S.op("dve", lambda e, SEG=SEG: e.scalar_tensor_tensor(out=t2a[:, 0:SEG], in0=a2[:, 0:SEG], scalar=0.0, in1=t2a[:, 0:SEG],
                                                                  op0=ALU.max, op1=ALU.add), reads=[r_a2, r_t2a], writes=[r_t2a])
            S.op("dve", lambda e, SEG=SEG: e.tensor_scalar(out=t2a[:, 0:SEG], in0=t2a[:, 0:SEG], scalar1=hv[:, 2:3], scalar2=None,
                                                           op0=ALU.mult), reads=[r_t2a, r_hv], writes=[r_t2a])
            S.op("act", lambda e, SEG=SEG: e.activation(out=t2b[:, 0:SEG], in_=b2[:, 0:SEG], func=AF.Sigmoid),
                 reads=[r_b2], writes=[r_t2b])
            yield

        def prep(dirn, c0, bp):
            mi = masks[:, dirn, :]
            ms = masks[:, 2 + dirn, :]
            gsel = gbt[:, :, dirn:dirn + 1]
            bsel = gbt[:, :, 2 + dirn:3 + dirn]
            v3 = lambda bank: bank[:].rearrange("p (a b) -> p a b", a=NB)
            f2 = lambda t: t[:].rearrange("p a b -> p (a b)")

            def trg(e):
                ins = None
                for b in range(NB):
                    cc = c0 + b * 128
                    e.transpose(out=Bk[0][:, b * 4:b * 4 + 2], in_=t2a[0:2, cc:cc + 128], identity=identf[0:2, 0:2])
                    ins = e.transpose(out=Bk[0][:, b * 4 + 2:b * 4 + 4], in_=t2b[0:2, cc:cc + 128], identity=identf[0:2, 0:2])
                return ins
            S.op("pe", trg, reads=[r_t2a, r_t2b], writes=[r_B[0]])
            S.op("dve", lambda e: e.tensor_copy(out=gbt[:].rearrange("p a b -> p (a b)"), in_=Bk[0][:, 0:NB * 4]),
                 reads=[r_B[0]], writes=[r_gbt])
            yield
            S.op("dve", lambda e: e.tensor_tensor(out=g_bc[:], in0=bcm(mi), in1=bcl(gsel), op=ALU.mult),
                 reads=[r_gbt, r_masks], writes=[r_g_bc])
            S.op("dve", lambda e: e.tensor_tensor(out=gi2[:], in0=masks[:, 6, 0:2].unsqueeze(1).broadcast_to([128, NB, 2]),
                                                  in1=gsel.broadcast_to([128, NB, 2]), op=ALU.mult),
                 reads=[r_gbt, r_masks], writes=[r_gi2])
            yield
            S.op("dve", lambda e: e.tensor_tensor(out=dgb[:], in0=bcm(identf[:]), in1=bcl(bsel), op=ALU.mult),
                 reads=[r_gbt], writes=[r_dgb])

            def mA(e):
                e.matmul(Bk[1][:], lhsT=ones_f, rhs=f2(g_bc), start=True, stop=True)
                e.matmul(Bk[0][:, 32:32 + NB], lhsT=mi, rhs=gbt[:, :, dirn], start=True, stop=True)
                e.matmul(Bk[0][:, 40:40 + NB], lhsT=masks[:, 5, :], rhs=gbt[:, :, dirn], start=True, stop=True)
                return e.matmul(Bk[0][:, 48:48 + 2 * NB], lhsT=ones_f, rhs=f2(gi2), start=True, stop=True)
            S.op("pe", mA, reads=[r_g_bc, r_gi2, r_masks, r_gbt], writes=[r_B[0], r_B[1]])
            yield
            S.op("dve", lambda e: e.tensor_copy(out=cols[:, :, 0], in_=Bk[0][:, 32:32 + NB]), reads=[r_B[0]], writes=[r_cols])
            S.op("dve", lambda e: e.tensor_copy(out=cols[:, :, 1], in_=Bk[0][:, 40:40 + NB]), reads=[r_B[0]], pwrites=[r_cols])
            S.op("act", lambda e: e.activation(out=f2(cdx[bp]), in_=Bk[0][:, 48:48 + 2 * NB], func=AF.Exp), reads=[r_B[0]], writes=[r_cdx[bp]])
            yield
            S.op("act", lambda e: e.activation(out=f2(Egc), in_=Bk[1][:], func=AF.Exp),
                 reads=[r_B[1]], writes=[r_Egc])
            S.op("dve", lambda e: e.tensor_tensor(out=Dm[:], in0=v3(Bk[1]), in1=bcl(cols[:, :, 0:1]),
                                                  op=ALU.subtract), reads=[r_B[1], r_cols], writes=[r_Dm])
            yield
            S.op("dve", lambda e: e.tensor_scalar(out=Dm[:], in0=Dm[:], scalar1=0.0, scalar2=None, op0=ALU.min), reads=[r_Dm], writes=[r_Dm])
            S.op("act", lambda e: e.activation(out=Dm[:], in_=Dm[:], func=AF.Exp), reads=[r_Dm], writes=[r_Dm])
            yield
            S.op("dve", lambda e: e.tensor_tensor(out=decI[:], in0=Dm[:], in1=bcm(mi), op=ALU.mult), reads=[r_Dm, r_masks], writes=[r_decI])
            S.op("dve", lambda e: e.tensor_tensor(out=decS[:], in0=Dm[:], in1=bcm(ms), op=ALU.mult), reads=[r_Dm, r_masks], writes=[r_decS])
            yield
            S.op("act", lambda e: e.activation(out=cols[:, :, 2:3], in_=cols[:, :, 0:1], func=AF.Exp), reads=[r_cols], pwrites=[r_cols])
            S.op("dve", lambda e: e.tensor_tensor(out=cols[:, :, 3:4], in0=cols[:, :, 1:2], in1=cols[:, :, 0:1], op=ALU.subtract),
                 reads=[r_cols], pwrites=[r_cols])
            yield
            S.op("act", lambda e: e.activation(out=cols[:, :, 3:4], in_=cols[:, :, 3:4], func=AF.Exp), reads=[r_cols], pwrites=[r_cols])
            S.op("dve", lambda e: e.tensor_tensor(out=cols[:, :, 4:5], in0=cols[:, :, 2:3], in1=bsel, op=ALU.mult),
                 reads=[r_cols, r_gbt], pwrites=[r_cols])
            yield

            def mB(e):
                for b in range(NB):
                    cc = c0 + b * 128
                    e.matmul(Bk[2][:, b * 128:(b + 1) * 128], lhsT=kTb[:, cc:cc + 128], rhs=kTb[:, cc:cc + 128], start=True, stop=True)
                    e.matmul(Bk[3][:, b * 128:(b + 1) * 128], lhsT=kTb[:, cc:cc + 128], rhs=qTb[:, cc:cc + 128], start=True, stop=True)
                return e.matmul(Bk[4][:], lhsT=ones_f, rhs=f2(dgb), start=True, stop=True)
            S.op("pe", mB, reads=[r_cv[0], r_cv[1], r_dgb, r_masks], writes=[r_B[2], r_B[3], r_B[4]])
            yield
            S.op("dve", lambda e: e.tensor_tensor(out=AQT[bp][:], in0=v3(Bk[3]), in1=decI[:], op=ALU.mult),
                 reads=[r_B[3], r_decI], writes=[r_AQT[bp]])
            S.op("dve", lambda e: e.tensor_tensor(out=decS[:], in0=v3(Bk[2]), in1=decS[:], op=ALU.mult),
                 reads=[r_B[2], r_decS], writes=[r_decS])
            yield
            S.op("dve", lambda e: e.scalar_tensor_tensor(out=X0f[:], in0=decS[:], scalar=-1.0, in1=v3(Bk[4]),
                                                         op0=ALU.mult, op1=ALU.mult), reads=[r_B[4], r_decS], writes=[r_X0f])
            yield

            def trX(e):
                ins = None
                for b in range(NB):
                    ins = e.transpose(out=Bk[0][:, b * 128:(b + 1) * 128], in_=X0f[:, b, :], identity=identf[:])
                return ins
            S.op("pe", trX, reads=[r_X0f], writes=[r_B[0]])
            S.op("act", lambda e: e.activation(out=f2(XT_[0]), in_=Bk[0][:], func=AF.Copy), reads=[r_B[0]], writes=[r_XT[0]])
            S.op("dve", lambda e: e.tensor_copy(out=XA_[0][:], in_=X0f[:]), reads=[r_X0f], writes=[r_XA[0]])
            yield
            S.op("dve", lambda e: e.tensor_tensor(out=Pm[:], in0=X0f[:], in1=bcm(identf[:]), op=ALU.add),
                 reads=[r_X0f], writes=[r_Pm])
            S.op("act", lambda e: e.activation(out=Pb[:], in_=Pm[:], func=AF.Copy), reads=[r_Pm], writes=[r_Pb])
            yield
            for lv in range(1, 6):
                s_, d_ = (lv - 1) % 2, lv % 2

                def sqm(e, s_=s_, lv=lv):
                    ins = None
                    for b in range(NB):
                        ins = e.matmul(Bk[2][:, b * 128:(b + 1) * 128], lhsT=XA_[s_][:, b, :], rhs=XT_[s_][:, b, :], start=True, stop=True)
                        if lv < 5:
                            ins = e.matmul(Bk[1][:, b * 128:(b + 1) * 128], lhsT=XT_[s_][:, b, :], rhs=XA_[s_][:, b, :], start=True, stop=True)
                    return ins
                S.op("pe", sqm, reads=[r_XA[s_], r_XT[s_]], writes=[r_B[1], r_B[2]])
                S.op("act", lambda e, d_=d_: e.activation(out=f2(XT_[d_]), in_=Bk[2][:], func=AF.Copy),
                     reads=[r_B[2]], writes=[r_XT[d_]])
                if lv < 5:
                    S.op("dve", lambda e, d_=d_: e.tensor_copy(out=f2(XA_[d_]), in_=Bk[1][:]),
                         reads=[r_B[1]], writes=[r_XA[d_]])
                yield

                def pmm(e, d_=d_):
                    ins = None
                    for b in range(NB):
                        ins = e.matmul(Bk[0][:, b * 128:(b + 1) * 128], lhsT=XT_[d_][:, b, :], rhs=Pb[:, b, :], start=True, stop=True)
                    return ins
                S.op("pe", pmm, reads=[r_XT[d_], r_Pb], writes=[r_B[0]])
                S.op("dve", lambda e: e.tensor_tensor(out=f2(Pm), in0=f2(Pm), in1=Bk[0][:], op=ALU.add), reads=[r_B[0], r_Pm], writes=[r_Pm])
                S.op("act", lambda e: e.activation(out=Pb[:], in_=Pm[:], func=AF.Copy), reads=[r_Pm], writes=[r_Pb])
                yield

            def trkv(e):
                ins = None
                for b in range(NB):
                    cc = c0 + b * 128
                    e.transpose(out=Bk[3][:, b * 128:(b + 1) * 128], in_=cv[1][:, cc:cc + 128], identity=identf[:])
                    ins = e.transpose(out=Bk[4][:, b * 128:(b + 1) * 128], in_=cv[2][:, cc:cc + 128], identity=identf[:])
                return ins
            S.op("pe", trkv, reads=[r_cv[1], r_cv[2]], writes=[r_B[3], r_B[4]])
            yield
            S.op("dve", lambda e: e.tensor_tensor(out=bv[:], in0=v3(Bk[4]), in1=bcl(bsel), op=ALU.mult),
                 reads=[r_B[4], r_gbt], writes=[r_bv])
            S.op("dve", lambda e: e.tensor_tensor(out=bke[:], in0=v3(Bk[3]), in1=bcl(cols[:, :, 4:5]), op=ALU.mult),
                 reads=[r_B[3], r_cols], writes=[r_bke])
            yield
            S.op("dve", lambda e: e.tensor_tensor(out=kdc[bp][:], in0=v3(Bk[3]), in1=bcl(cols[:, :, 3:4]), op=ALU.mult),
                 reads=[r_B[3], r_cols], writes=[r_kdc[bp]])
            S.op("dve", lambda e: e.tensor_tensor(out=qdT[bp][:].rearrange("p a b -> p (a b)"), in0=cv[0][:, c0:c0 + W],
                                                  in1=Egc[:].rearrange("p a b -> p (a b)"), op=ALU.mult),
                 reads=[r_cv[0], r_Egc], writes=[r_qdT[bp]])
            yield

            def muw(e):
                ins = None
                for b in range(NB):
                    e.matmul(Bk[1][:, b * 128:(b + 1) * 128], lhsT=Pb[:, b, :], rhs=bv[:, b, :], start=True, stop=True)
                    ins = e.matmul(Bk[2][:, b * 128:(b + 1) * 128], lhsT=bke[:, b, :], rhs=Pb[:, b, :], start=True, stop=True)
                return ins
            S.op("pe", muw, reads=[r_Pb, r_bv, r_bke], writes=[r_B[1], r_B[2]])
            S.op("dve", lambda e: e.tensor_copy(out=u_t[bp][:].rearrange("p a b -> p (a b)"), in_=Bk[1][:]), reads=[r_B[1]], writes=[r_u[bp]])
            S.op("act", lambda e: e.activation(out=wT[bp][:].rearrange("p a b -> p (a b)"), in_=Bk[2][:], func=AF.Copy),
                 reads=[r_B[2]], writes=[r_wT[bp]])
            yield

        def scan(dirn, c0, bp, sp, gt_base, last_of_seg, seg_dma):
            order = range(NB) if dirn == 0 else range(NB - 1, -1, -1)
            for b in order:
                ob = b % 2
                for ch in ((0, 1) if dirn == 0 else (1, 0)):
                    cs = slice(ch * 64, ch * 64 + 64)
                    S.op("pe", lambda e, cs=cs, b=b: e.matmul(pS[cs, 0:128], lhsT=wT[bp][:, b, cs], rhs=Sb[:], start=True, stop=True),
                         reads=[r_wT[bp], r_Sb], writes=[r_pS])
                    S.op("dve", lambda e, cs=cs, b=b: e.tensor_tensor(out=vn[cs, :], in0=u_t[bp][cs, b, :], in1=pS[cs, 0:128], op=ALU.subtract),
                         reads=[r_pS, r_u[bp]], writes=[r_vn])
                    yield

                    S.op("pe", lambda e, cs=cs, b=b: e.matmul(pD[:, 0:128], lhsT=kdc[bp][cs, b, :], rhs=vn[cs, :], start=True, stop=True),
                         reads=[r_kdc[bp], r_vn], writes=[r_pD])

                    def mo(e, cs=cs, b=b):
                        e.matmul(pS[cs, 128:256], lhsT=qdT[bp][:, b, cs], rhs=Sb[:], start=True, stop=False)
                        return e.matmul(pS[cs, 128:256], lhsT=AQT[bp][cs, b, cs], rhs=vn[cs, :], start=False, stop=True)
                    S.op("pe", mo, reads=[r_qdT[bp], r_Sb, r_AQT[bp], r_vn], writes=[r_pS])
                    yield
                    S.op("dve", lambda e, ch=ch, b=b: e.scalar_tensor_tensor(out=Sst[:], in0=Sst[:], scalar=cdx[bp][:, b, ch:ch + 1], in1=pD[:, 0:128],
                                                                           op0=ALU.mult, op1=ALU.add), reads=[r_pD, r_cdx[bp], r_S], writes=[r_S])
                    S.op("dve", lambda e: e.tensor_copy(out=Sb[:], in_=Sst[:]), reads=[r_S], writes=[r_Sb])
                    S.op("act", lambda e, cs=cs, ob=ob: e.activation(out=o_t[ob][cs, :], in_=pS[cs, 128:256], func=AF.Copy),
                         reads=[r_pS], pwrites=[r_o[ob]])
                    yield
                gt0 = gt_base + c0 + b * 128
                cc = c0 + b * 128
                if dirn == 0:
                    S.op("sp", lambda e, gt0=gt0, ob=ob: e.dma_start(out=of_d[gt0:gt0 + 128, :], in_=o_t[ob][:]), reads=[r_o[ob]], pwrites=[C.r_of], dma=True)
                else:
                    S.op("sp", lambda e, gt0=gt0: e.dma_start(out=of_t[:], in_=of_d[gt0:gt0 + 128, :]), reads=[C.r_of], writes=[r_of], dma=True)
                    S.op("dve", lambda e, ob=ob: e.tensor_tensor(out=of_t[:], in0=of_t[:], in1=o_t[ob][:], op=ALU.add), reads=[r_o[ob], r_of], writes=[r_of])
                    yield
                    S.op("dve", lambda e: e.tensor_tensor(out=osq[:], in0=of_t[:], in1=of_t[:], op=ALU.mult), reads=[r_of], writes=[r_osq])
                    S.op("dve", lambda e: e.tensor_reduce(out=oss[:], in_=osq[:], axis=AX.X, op=ALU.add), reads=[r_osq], writes=[r_oss])
                    yield
                    S.op("dve", lambda e: e.tensor_scalar(out=oss[:], in0=oss[:], scalar1=1.0 / 128, scalar2=EPS, op0=ALU.mult, op1=ALU.add),
                         reads=[r_oss], writes=[r_oss])
                    S.op("act", lambda e: e.activation(out=oss[:], in_=oss[:], func=AF.Sqrt), reads=[r_oss], writes=[r_oss])
                    yield
                    S.op("dve", lambda e: e.reciprocal(out=oss[:], in_=oss[:]), reads=[r_oss], writes=[r_oss])
                    S.op("dve", lambda e: e.scalar_tensor_tensor(out=on_t[:], in0=of_t[:], scalar=oss[:, 0:1], in1=gbc[:],
                                                                 op0=ALU.mult, op1=ALU.mult), reads=[r_of, r_oss, r_gbc], writes=[r_on])
                    yield
                    S.op("pe", lambda e: e.transpose(out=pN[:, 0:128], in_=on_t[:], identity=identf[:]), reads=[r_on], writes=[r_pN])
                    S.op("dve", lambda e, cc=cc: e.tensor_tensor(out=ogs[sp][:, cc:cc + 128], in0=pN[:, 0:128], in1=szs[sp][:, cc:cc + 128], op=ALU.mult),
                         reads=[r_pN, r_szs[sp]], pwrites=[r_ogs[sp]])
                yield
            if last_of_seg and dirn == 1:
                seg_dma()
                yield

        def interleave(gens):
            gens = [g for g in gens if g is not None]
            while gens:
                for g in list(gens):
                    try:
                        next(g)
                    except StopIteration:
                        gens.remove(g)

        def chain(*gs):
            for g in gs:
                yield from g

        segn = 0
        bn = 0
        for part, SEG, L, gbase in (("P", TP, LP, 0), ("S", TS, LS, LP)):
            ogo, _ = OGR[part]
            og_v = fog[ogo:ogo + 128 * L].rearrange("(f t) -> f t", f=128)
            for dirn in (0, 1):
                S.op("dve", lambda e: e.memset(Sst[:], 0.0), writes=[r_S])
                S.op("dve", lambda e: e.memset(Sb[:], 0.0), writes=[r_Sb])
                pending = None
                seg_order = range(NCORE) if dirn == 0 else range(NCORE - 1, -1, -1)
                for r in seg_order:
                    sp = segn % 2; segn += 1
                    nbat = SEG // W
                    bat_order = range(nbat) if dirn == 0 else range(nbat - 1, -1, -1)
                    first = True
                    for bi_, bi in enumerate(bat_order):
                        bp = bn % 2; bn += 1
                        c0 = bi * W
                        pg = prep(dirn, c0, bp)
                        if first:
                            pg = chain(seg_load(part, SEG, dirn, r, sp), pg)
                            first = False
                        if GDN_INTERLEAVE:
                            interleave([pending, pg])
                        else:
                            interleave([pending])
                            interleave([pg])

                        def seg_dma(r=r, SEG=SEG, sp=sp, og_v=og_v):
                            gs = r * SEG
                            S.op("sp", lambda e: e.dma_start(out=og_v[:, gs:gs + SEG], in_=ogs[sp][:, 0:SEG]),
                                 reads=[r_ogs[sp]], pwrites=[C.r_ogx], dma=True)
                        pending = scan(dirn, c0, bp, sp, gbase + r * SEG, bi_ == nbat - 1, seg_dma)
                interleave([pending])
        S.barrier()
        S.flush()


def emit_wout(C, nc, OG_g, mixT_d, h1_d, h2_d, w_out, idx_t, r_idx, LP, LS):
    S = C.S
    TP, TS = LP // NCORE, LS // NCORE
    T = TP + TS
    KI = idx_layout(LP, LS)
    fog = flat(OG_g).rearrange("(m f) -> m f", f=512)
    with ExitStack() as st:
        mixT = _tile(st, nc, "w_mixT", [128, 16, 512], BF16); r_mixT = Res()
        ws = [_tile(st, nc, "w_ws%d" % i, [128, KD, 256], BF16) for i in range(2)]
        r_ws = [Res() for _ in range(2)]
        ht = [_tile(st, nc, "w_ht%d" % i, [128, D], F32) for i in range(4)]
        r_ht = [Res() for _ in range(4)]
        pw = [_psum(st, nc, "w_pw%d" % i, [128, 512], F32) for i in range(4)]
        r_pw = [Res(excl=True) for _ in range(4)]
        wrr = 0
        pc = 0
        for g in range(T // 512):
            t0 = g * 512
            for h in range(8):
                col = KI[("og", g, h)]
                S.op("pool", lambda e, h=h, col=col: e.indirect_dma_start(
                    out=mixT[:, h, :], out_offset=None, in_=fog,
                    in_offset=bass.IndirectOffsetOnAxis(ap=idx_t[0:128, col:col + 1], axis=0)),
                    reads=[C.r_ogg, r_idx], pwrites=[r_mixT], dma=True)
            S.op("sp", lambda e, t0=t0: e.dma_start(out=mixT[:, 8:16, :], in_=mixT_d[:, :, t0:t0 + 512].rearrange("h f t -> f h t")),
                 reads=[C.r_mixA], pwrites=[r_mixT], dma=True)
            for i in range(4):
                S.op("sp", lambda e, i=i, t0=t0: e.dma_start(out=ht[i][:], in_=h1_d[t0 + i * 128:t0 + (i + 1) * 128, :]),
                     reads=[C.r_h1], writes=[r_ht[i]], dma=True)
            for dc in range(8):
                wb = wrr % 2; wrr += 1
                S.op("pool", lambda e, wb=wb, dc=dc: e.dma_start(
                    out=ws[wb][:], in_=w_out[:, dc * 256:(dc + 1) * 256].rearrange("(k p) c -> p k c", p=128)),
                    writes=[r_ws[wb]], dma=True)
                for i in range(4):
                    pi = pc % 4; pc += 1

                    def mm(e, wb=wb, i=i, pi=pi):
                        ins = None
                        for k in range(KD):
                            ins = e.matmul(pw[pi][:, 0:256], lhsT=mixT[:, k, i * 128:(i + 1) * 128], rhs=ws[wb][:, k, :],
                                           start=(k == 0), stop=(k == KD - 1))
                        return ins
                    S.op("pe", mm, reads=[r_ws[wb], r_mixT], writes=[r_pw[pi]])
                    S.op("dve", lambda e, i=i, dc=dc, pi=pi: e.tensor_tensor(
                        out=ht[i][:, dc * 256:(dc + 1) * 256], in0=ht[i][:, dc * 256:(dc + 1) * 256], in1=pw[pi][:, 0:256], op=ALU.add),
                        reads=[r_pw[pi]], writes=[r_ht[i]])
            for i in range(4):
                S.op("sp", lambda e, i=i, t0=t0: e.dma_start(out=h2_d[t0 + i * 128:t0 + (i + 1) * 128, :], in_=ht[i][:]),
                     reads=[r_ht[i]], pwrites=[C.r_h2], dma=True)
        S.barrier()
        S.flush()


def emit_final(C, nc, h3_d, y_d, fnorm_d, T):
    S = C.S
    with ExitStack() as st:
        gb = _tile(st, nc, "z_gb", [128, D], F32); r_gb = Res()
        xt = [_tile(st, nc, "z_xt%d" % i, [128, D], F32) for i in range(2)]
        r_xt = [Res() for _ in range(2)]
        yt = [_tile(st, nc, "z_yt%d" % i, [128, D], F32) for i in range(2)]
        r_yt = [Res() for _ in range(2)]
        junk = _tile(st, nc, "z_junk", [128, D], BF16); r_junk = Res()
        ss = _tile(st, nc, "z_ss", [128, 2], F32); r_ss = [Res() for _ in range(2)]
        S.op("sp", lambda e: e.dma_start(out=gb[:], in_=fnorm_d.partition_broadcast(128)), writes=[r_gb], dma=True)
        for i in range(T // 128):
            b = i % 2
            S.op("sp", lambda e, b=b, i=i: e.dma_start(out=xt[b][:], in_=h3_d[i * 128:(i + 1) * 128, :]),
                 reads=[C.r_h3], writes=[r_xt[b]], dma=True)
            S.op("act", lambda e, b=b: e.activation(out=junk[:], in_=xt[b][:], func=AF.Square, accum_out=ss[:, b:b + 1]),
                 reads=[r_xt[b]], writes=[r_junk, r_ss[b]])
            S.op("dve", lambda e, b=b: e.tensor_scalar(out=ss[:, b:b + 1], in0=ss[:, b:b + 1], scalar1=1.0 / D, scalar2=EPS,
                                                       op0=ALU.mult, op1=ALU.add), reads=[r_ss[b]], writes=[r_ss[b]])
            S.op("act", lambda e, b=b: e.activation(out=ss[:, b:b + 1], in_=ss[:, b:b + 1], func=AF.Sqrt), reads=[r_ss[b]], writes=[r_ss[b]])
            S.op("dve", lambda e, b=b: e.reciprocal(out=ss[:, b:b + 1], in_=ss[:, b:b + 1]), reads=[r_ss[b]], writes=[r_ss[b]])
            S.op("dve", lambda e, b=b: e.scalar_tensor_tensor(out=yt[b][:], in0=xt[b][:], scalar=ss[:, b:b + 1], in1=gb[:],
                                                              op0=ALU.mult, op1=ALU.mult), reads=[r_xt[b], r_ss[b], r_gb], writes=[r_yt[b]])
            S.op("sp", lambda e, b=b, i=i: e.dma_start(out=y_d[i * 128:(i + 1) * 128, :], in_=yt[b][:]),
                 reads=[r_yt[b]], pwrites=[C.r_y], dma=True)
        S.barrier()
        S.flush()


def _pad2048(n):
    return (n + 2047) // 2048 * 2048


def build_program(LP, LS, stop_after=None):
    TP, TS = LP // NCORE, LS // NCORE
    T = TP + TS
    LT = LP + LS
    nc = bass.Bass("TRN2", target_bir_lowering=False)

    def din(name, shape, dt=F32):
        return nc.dram_tensor(name, list(shape), dt, kind="ExternalInput").ap()

    def dint(name, shape, dt=F32):
        return nc.dram_tensor(name, list(shape), dt).ap()

    x_d = din("x", [T, D])
    w1g, w1u, w1d = din("w1g", [D, DFF]), din("w1u", [D, DFF]), din("w1d", [DFF, D])
    w2g, w2u, w2d = din("w2g", [D, DFF]), din("w2u", [D, DFF]), din("w2d", [DFF, D])
    w_in, w_out = din("w_in", [D, IN_DIM]), din("w_out", [D, D])
    gains = din("gains", [4, 128, KD])
    consts = din("consts", [128, 128])
    KI = idx_layout(LP, LS)
    idx_d = din("idx", [128, len(KI)], I32)
    vecs_d = din("vecs", [8, 128])
    hv_d = din("hv", [2, 4])
    convw_d = din("convw", [128, 3, 5])
    masks_d = din("masks", [128, 7, 128])
    rope_d = din("rope", [T, 128])
    fnorm_d = din("fnorm", [D])
    y_d = nc.dram_tensor("y", [T, D], F32, kind="ExternalOutput").ap()
    h1_d, h2_d, h3_d = dint("h1_s", [T, D]), dint("h2_s", [T, D]), dint("h3_s", [T, D])
    RA, NA, RB, NB = region_sizes(TP, TS)
    NAp, NBp = {p: _pad2048(NA[p]) for p in NA}, _pad2048(NB)
    OGR, NO = og_sizes(LP, LS)
    NOp = _pad2048(NO)
    XA_x = {p: dint("XA_x" + p, [NAp[p] // 2048, 2048]) for p in NAp}
    XA_g = {p: dint("XA_g" + p, [NCORE * NAp[p] // 2048, 2048]) for p in NAp}
    XB_x, XB_g = dint("XB_x", [NBp // 2048, 2048], BF16), dint("XB_g", [NCORE * NBp // 2048, 2048], BF16)
    OG_x, OG_g = dint("OG_x", [NOp // 2048, 2048], BF16), dint("OG_g", [NCORE * NOp // 2048, 2048], BF16)
    qT_d = dint("qT_s", [8, 128, T], BF16)
    mixT_d = dint("mixT_s", [8, 128, T], BF16)
    of_d = dint("of_s", [LT, 128])

    C = Ctx()
    C.NAp, C.NBp, C.NOp = NAp, NBp, NOp
    C.pending_cc = []
    with ExitStack() as top:
        S = Sched(nc, top, 8, 8)
        C.S = S
        for i in range(4):
            S.sem["cc%d" % i] = top.enter_context(nc.semaphore("cc%d" % i))
            S.cnt["cc%d" % i] = 0
        C.ident_f = _tile(top, nc, "ident_f", [128, 128], F32)
        C.ident_b = _tile(top, nc, "ident_b", [128, 128], BF16)
        idx_t = _tile(top, nc, "idx_t", [128, len(KI)], I32)
        r_const = Res(); r_idx = Res()
        for nm in ("r_xch", "r_xchg", "r_qT", "r_mixA", "r_of", "r_ogx", "r_ogg", "r_h1", "r_h2", "r_h3", "r_y"):
            setattr(C, nm, Res())
        S.block = top.enter_context(nc.Block())
        S.op("sp", lambda e: e.dma_start(out=C.ident_f[:], in_=consts), writes=[r_const], dma=True)
        S.op("sp", lambda e: e.dma_start(out=idx_t[:], in_=idx_d), writes=[r_idx], dma=True)
        S.op("dve", lambda e: e.tensor_copy(out=C.ident_b[:], in_=C.ident_f[:]), reads=[r_const], writes=[r_const])
        S.barrier()

        def allgather(i, src, dst, wait=True):
            key = "cc%d" % i
            deps = {k: v for k, v in S.all_events().items() if not k.startswith("cc")}
            S._wait("pool", deps)
            sem = S.sem[key]
            S.lists["pool"].append(lambda e: e.collective_compute(
                "AllGather", ALU.bypass, replica_groups=[list(range(NCORE))], ins=[src.opt()], outs=[dst.opt()]).then_inc(sem, 1))
            if wait:
                S.cnt[key] += 1
                S.barrier()
            else:
                C.pending_cc.append(key)

        if stop_after == "ffn1":
            emit_ffn(C, nc, x_d, y_d, gains[0], w1g, w1u, w1d, T)
        else:
            emit_ffn(C, nc, x_d, h1_d, gains[0], w1g, w1u, w1d, T)
            emit_proj(C, nc, h1_d, gains[1], w_in, XA_x, XB_x, qT_d, rope_d, vecs_d, T, TP, TS)
            allgather(0, XB_x, XB_g)
            allgather(1, XA_x["P"], XA_g["P"], wait=False)
            allgather(3, XA_x["S"], XA_g["S"], wait=False)
            emit_attn(C, nc, XB_g, qT_d, mixT_d, vecs_d, T, TP, TS)
            for key in C.pending_cc:
                S.cnt[key] += 1
            C.pending_cc = []
            S.barrier()
            emit_gdn(C, nc, XA_g, XB_g, OG_x, of_d, idx_t, r_idx, hv_d, vecs_d, convw_d, masks_d, LP, LS)
            allgather(2, OG_x, OG_g)
            if stop_after == "h2":
                emit_wout(C, nc, OG_g, mixT_d, h1_d, y_d, w_out, idx_t, r_idx, LP, LS)
            else:
                emit_wout(C, nc, OG_g, mixT_d, h1_d, h2_d, w_out, idx_t, r_idx, LP, LS)
                emit_ffn(C, nc, h2_d, h3_d, gains[2], w2g, w2u, w2d, T)
                emit_final(C, nc, h3_d, y_d, fnorm_d, T)
        S.barrier()
        S.flush()
    return nc


_PROG_CACHE = {}


def _gain_cols(g):
    return np.ascontiguousarray(np.asarray(g, np.float32).reshape(KD, 128).T)


def _masks():
    t = np.arange(128)
    same = (t[:, None] // 64) == (t[None, :] // 64)
    m = np.zeros((128, 7, 128), np.float32)
    m[:, 0] = same & (t[:, None] <= t[None, :])
    m[:, 1] = same & (t[:, None] >= t[None, :])
    m[:, 2] = same & (t[:, None] < t[None, :])
    m[:, 3] = same & (t[:, None] > t[None, :])
    m[:, 4] = 1.0
    m[:, 5] = same
    m[:, 6, 0] = t < 64
    m[:, 6, 1] = t >= 64
    return m


def _rope_table(pos):
    pos = np.asarray(pos)
    row = (pos // 64).astype(np.float32)
    col = (pos % 64).astype(np.float32)
    freqs = (np.float32(10000.0) ** (-np.arange(0, 64, 2, dtype=np.float32) / np.float32(64))).astype(np.float32)
    ar = (row[:, None] * freqs[None, :]).astype(np.float32)
    ac = (col[:, None] * freqs[None, :]).astype(np.float32)
    return np.concatenate([np.cos(ar), np.cos(ac), np.sin(ar), np.sin(ac)], 1).astype(np.float32)


def _idx_table(c, LP, LS):
    TP, TS = LP // NCORE, LS // NCORE
    RA, NA, RB, NB = region_sizes(TP, TS)
    NApd, NBp = {p: _pad2048(NA[p]) for p in NA}, _pad2048(NB)
    OGR, NO = og_sizes(LP, LS)
    NOp = _pad2048(NO)
    KI = idx_layout(LP, LS)
    tab = np.zeros((128, len(KI)), np.int64)
    p = np.arange(128)
    for key, col in KI.items():
        if key[0] == "q":
            _, part, r, j, kind = key
            SEG = TP if part == "P" else TS
            off = RA["qkv" + part][0]
            NAp = NApd[part]
            base = off + ((c * 3 + j) * 128 + p) * SEG
            if kind == "m":
                tab[:, col] = (r * NAp + base) // SEG
            elif kind == "l":
                tab[:, col] = ((max(r - 1, 0)) * NAp + base + SEG - 2) // 2
            else:
                tab[:, col] = ((min(r + 1, NCORE - 1)) * NAp + base) // 2
        elif key[0] == "sz":
            _, part, r = key
            SEG = TP if part == "P" else TS
            tab[:, col] = (r * NBp + RB["sz" + part][0] + (c * 128 + p) * SEG) // SEG
        elif key[0] in ("a", "b"):
            _, part, r = key
            SEG = TP if part == "P" else TS
            rowb = 0 if key[0] == "a" else 16
            pp = np.minimum(p, 1)
            tab[:, col] = (r * NApd[part] + RA["ab" + part][0] + (rowb + pp * 8 + c) * SEG) // SEG
        else:
            _, g, h = key
            t0 = g * 512
            if t0 < TP:
                L, Tp, o, off = LP, TP, t0, OGR["P"][0]
            else:
                L, Tp, o, off = LS, TS, t0 - TP, OGR["S"][0]
            tab[:, col] = (h * NOp + off + p * L + c * Tp + o) // 512
    assert tab.max() < 2 ** 31
    return tab.astype(np.int32)


def run(inp, LP, LS, stop_after=None):
    TP, TS = LP // NCORE, LS // NCORE
    key = (LP, LS, stop_after)
    if key not in _PROG_CACHE:
        _PROG_CACHE[key] = build_program(LP, LS, stop_after)
    nc = _PROG_CACHE[key]
    f = lambda a: np.ascontiguousarray(np.asarray(a, np.float32))
    xp, xs = f(inp["x_prompt"])[0], f(inp["x_sample"])[0]
    vecs = np.zeros((8, 128), np.float32)
    vecs[0], vecs[1], vecs[2], vecs[3] = f(inp["q_norm"])[0], f(inp["k_norm"])[0], f(inp["attn_out_norm"])[0], f(inp["gdn_out_norm"])[0]
    cwf = f(inp["conv_w"])[0]
    shared = {
        "w1g": f(inp["ffn1_w_gate"])[0], "w1u": f(inp["ffn1_w_up"])[0], "w1d": f(inp["ffn1_w_down"])[0],
        "w2g": f(inp["ffn2_w_gate"])[0], "w2u": f(inp["ffn2_w_up"])[0], "w2d": f(inp["ffn2_w_down"])[0],
        "w_in": f(inp["w_in"])[0], "w_out": f(inp["w_out"])[0],
        "gains": np.stack([_gain_cols(inp["ffn1_norm"][0]), _gain_cols(inp["mix_norm"][0]),
                           _gain_cols(inp["ffn2_norm"][0]), _gain_cols(inp["final_norm"])]),
        "consts": np.eye(128, dtype=np.float32), "vecs": vecs, "masks": _masks(), "fnorm": f(inp["final_norm"]),
    }
    in_maps = []
    for c in range(NCORE):
        m = dict(shared)
        m["x"] = np.ascontiguousarray(np.concatenate([xp[c * TP:(c + 1) * TP], xs[c * TS:(c + 1) * TS]], 0))
        if stop_after != "ffn1":
            m["idx"] = _idx_table(c, LP, LS)
            hv = np.zeros((2, 4), np.float32)
            hv[0, 0], hv[0, 1] = f(inp["a_log_fwd"])[0, c], f(inp["dt_bias_fwd"])[0, c]
            hv[1, 0], hv[1, 1] = f(inp["a_log_bwd"])[0, c], f(inp["dt_bias_bwd"])[0, c]
            m["hv"] = hv
            m["convw"] = np.ascontiguousarray(cwf.reshape(5, 3, 8, 128)[:, :, c, :].transpose(2, 1, 0))
            m["rope"] = np.concatenate([_rope_table(np.arange(c * TP, (c + 1) * TP)), _rope_table(np.arange(c * TS, (c + 1) * TS))], 0)
        in_maps.append(m)
    if stop_after == "ffn1":
        keep = ("x", "w1g", "w1u", "w1d", "gains", "consts")
    else:
        keep = None
    if keep is not None:
        in_maps = [{k: v for k, v in m.items() if k in keep} for m in in_maps]
    res = run_bass_kernel_spmd(nc, in_maps, core_ids=list(range(NCORE)))
    ys = [np.asarray(r["y"]) for r in res.results]
    yp = np.concatenate([y[:TP] for y in ys], 0)[None]
    ysm = np.concatenate([y[TP:] for y in ys], 0)[None]
    return (yp.astype(np.float32), ysm.astype(np.float32))


def kernel(**inputs):
    return run(inputs, 16384, 8192)
```
